# Optimizing a Trainium2 kernel written in Bass

```python
import math
import jax
import jax.numpy as jnp
from jax import lax
import numpy as np

D_MODEL = 1024
BATCH = 4
SEQ = 4096
DEPTH = 4

GRID_W = 64
MEM_LEN = 256
ML_HEADS = 4
ML_WIDTH = D_MODEL // 2
ML_DIM = ML_WIDTH // ML_HEADS
ML_CHUNK = 128
NA_HEADS = 8
NA_WIDTH = D_MODEL - ML_WIDTH
NA_DIM = NA_WIDTH // NA_HEADS
NA_ROWS = 8
NA_COLS = 16
XA_HEADS = 4
XA_DIM = D_MODEL // XA_HEADS
D_FF = 128 * ((8 * D_MODEL // 3 + 127) // 128)
IN_COLS = 4 * ML_WIDTH + 4 * ML_HEADS + 3 * NA_WIDTH
DN_ALPHA = (2.0 * DEPTH) ** 0.25
DN_BETA = (8.0 * DEPTH) ** -0.25
LN_EPS = 1e-5

kernel_name = "hybrid_mlstm_natten_encoder"


def layer_norm(x, w, b):
    xf = x.astype(jnp.float32)
    mu = jnp.mean(xf, axis=-1, keepdims=True)
    var = jnp.mean(jnp.square(xf - mu), axis=-1, keepdims=True)
    y = (xf - mu) * lax.rsqrt(var + LN_EPS) * w.astype(jnp.float32) + b.astype(jnp.float32)
    return y.astype(x.dtype)


def mlstm_scan(q, k, v, ig, fg):
    b, h, t, dh = q.shape
    nc = t // ML_CHUNK

    def chunks(a):
        a = a.reshape((b, h, nc, ML_CHUNK) + a.shape[3:])
        return jnp.moveaxis(a, 2, 0)

    logf = jax.nn.log_sigmoid(fg)
    tril = jnp.tril(jnp.ones((ML_CHUNK, ML_CHUNK), dtype=bool))

    def step(carry, inp):
        c_mat, n_vec, m = carry
        qc, kc, vc, ic, lfc = inp
        a = jnp.cumsum(lfc, axis=-1)
        d_log = jnp.where(tril, a[..., :, None] - a[..., None, :] + ic[..., None, :], -jnp.inf)
        inter_log = a + m[..., None]
        m_t = jnp.maximum(inter_log, jnp.max(d_log, axis=-1))
        w_qk = jnp.exp(d_log - m_t[..., None]) * jnp.einsum('bhtd,bhsd->bhts', qc, kc)
        inter = jnp.exp(inter_log - m_t)
        num = jnp.einsum('bhts,bhsd->bhtd', w_qk, vc) + inter[..., None] * jnp.einsum('bhtd,bhde->bhte', qc, c_mat)
        den = jnp.sum(w_qk, axis=-1) + inter * jnp.einsum('bhtd,bhd->bht', qc, n_vec)
        h_out = num / jnp.maximum(jnp.abs(den), jnp.exp(-m_t))[..., None]
        a_end = a[..., -1]
        up_log = a_end[..., None] - a + ic
        m_new = jnp.maximum(a_end + m, jnp.max(up_log, axis=-1))
        kw = kc * jnp.exp(up_log - m_new[..., None])[..., None]
        decay = jnp.exp(a_end + m - m_new)
        c_new = decay[..., None, None] * c_mat + jnp.einsum('bhsd,bhse->bhde', kw, vc)
        n_new = decay[..., None] * n_vec + jnp.sum(kw, axis=2)
        return (c_new, n_new, m_new), h_out

    init = (jnp.zeros((b, h, dh, dh), jnp.float32),
            jnp.zeros((b, h, dh), jnp.float32),
            jnp.zeros((b, h), jnp.float32))
    _, hs = lax.scan(step, init, (chunks(q), chunks(k), chunks(v), chunks(ig), chunks(logf)))
    return jnp.moveaxis(hs, 0, 2).reshape(b, h, t, dh)


def neighborhood_attention(nq, nk, nv, rpb):
    b, t, _ = nq.shape
    rows = t // GRID_W
    kr = min(NA_ROWS, rows)

    def grid(a):
        return a.reshape(b, rows, GRID_W, NA_HEADS, NA_DIM).transpose(0, 3, 1, 2, 4).astype(jnp.float32)

    qg, kg, vg = grid(nq), grid(nk), grid(nv)
    r = jnp.arange(rows)
    row_idx = jnp.clip(r - kr // 2, 0, rows - kr)[:, None] + jnp.arange(kr)[None, :]
    k_rows = kg[:, :, row_idx].reshape(b, NA_HEADS, rows, kr * GRID_W, NA_DIM)
    v_rows = vg[:, :, row_idx].reshape(b, NA_HEADS, rows, kr * GRID_W, NA_DIM)
    s = jnp.einsum('bhrqd,bhrkd->bhrqk', qg, k_rows) * (NA_DIM ** -0.5)
    j = jnp.arange(GRID_W)
    cs = jnp.clip(j - NA_COLS // 2, 0, GRID_W - NA_COLS)
    col_ok = (j[None, :] >= cs[:, None]) & (j[None, :] < cs[:, None] + NA_COLS)
    dc_idx = jnp.clip(j[None, :] - j[:, None] + NA_COLS - 1, 0, 2 * NA_COLS - 2)
    dr_idx = row_idx - r[:, None] + NA_ROWS - 1
    bias = rpb[:, dr_idx[:, None, :, None], dc_idx[None, :, None, :]]
    bias = bias.reshape(NA_HEADS, rows, GRID_W, kr * GRID_W).astype(jnp.float32)
    mask = jnp.broadcast_to(col_ok[:, None, :], (GRID_W, kr, GRID_W)).reshape(GRID_W, kr * GRID_W)
    s = jnp.where(mask, s + bias, -jnp.inf)
    p = jax.nn.softmax(s, axis=-1)
    o = jnp.einsum('bhrqk,bhrkd->bhrqd', p, v_rows)
    return o.transpose(0, 2, 3, 1, 4).reshape(b, t, NA_WIDTH)


def hybrid_mixer(x, w_in, b_in, ml_norm_w, na_rpb, w_out, b_out):
    b, t, _ = x.shape
    z = x @ w_in + b_in
    sizes = [ML_WIDTH] * 4 + [ML_HEADS] * 4 + [NA_WIDTH] * 3
    offs = [int(o) for o in np.cumsum(sizes)[:-1]]
    mq, mk, mv, mo, i_f, f_f, i_b, f_b, nq, nk, nv = jnp.split(z, offs, axis=-1)

    def heads(a):
        return a.reshape(b, t, ML_HEADS, ML_DIM).transpose(0, 2, 1, 3).astype(jnp.float32)

    def gate(a):
        return a.transpose(0, 2, 1).astype(jnp.float32)

    q, k, v = heads(mq), heads(mk) * (ML_DIM ** -0.5), heads(mv)
    h_fwd = mlstm_scan(q, k, v, gate(i_f), gate(f_f))
    flip = lambda a: jnp.flip(a, axis=2)
    h_bwd = flip(mlstm_scan(flip(q), flip(k), flip(v), flip(gate(i_b)), flip(gate(f_b))))
    hm = h_fwd + h_bwd
    mu = jnp.mean(hm, axis=-1, keepdims=True)
    var = jnp.mean(jnp.square(hm - mu), axis=-1, keepdims=True)
    hn = ((hm - mu) * lax.rsqrt(var + LN_EPS)).transpose(0, 2, 1, 3).reshape(b, t, ML_WIDTH)
    ml_out = jax.nn.sigmoid(mo.astype(jnp.float32)) * (hn * ml_norm_w.astype(jnp.float32))
    na_out = neighborhood_attention(nq, nk, nv, na_rpb)
    y = jnp.concatenate([ml_out, na_out], axis=-1).astype(x.dtype)
    return y @ w_out + b_out


def memory_xattn(x, mem, w_q, w_kv, w_o, b_o):
    b, t, d = x.shape
    m = mem.shape[1]
    q = (x @ w_q).reshape(b, t, XA_HEADS, XA_DIM).astype(jnp.float32)
    kv = mem @ w_kv
    k, v = jnp.split(kv, 2, axis=-1)
    k = k.reshape(b, m, XA_HEADS, XA_DIM).astype(jnp.float32)
    v = v.reshape(b, m, XA_HEADS, XA_DIM).astype(jnp.float32)
    s = jnp.einsum('bthd,bmhd->bhtm', q, k) * (XA_DIM ** -0.5)
    p = jax.nn.softmax(s, axis=-1)
    o = jnp.einsum('bhtm,bmhd->bthd', p, v).reshape(b, t, d).astype(x.dtype)
    return o @ w_o + b_o


def conv_ffn(x, w_up, b_up, w_dw, b_dw, w_down, b_down):
    h = x @ w_up + b_up
    h = lax.conv_general_dilated(h, w_dw[:, None, :], window_strides=(1,), padding='SAME',
                                 dimension_numbers=('NWC', 'WIO', 'NWC'),
                                 feature_group_count=2 * D_FF) + b_dw
    g, u = jnp.split(h, 2, axis=-1)
    y = jax.nn.gelu(g, approximate=False) * u
    return y @ w_down + b_down


def setup_inputs(seed: int = 0) -> dict:
    key = jax.random.key(seed)
    ks = jax.random.split(key, 28)
    L = DEPTH
    nrm = lambda k, shape: jax.random.normal(k, shape, jnp.float32)
    fb = jnp.linspace(3.0, 6.0, ML_HEADS, dtype=jnp.float32)
    f_fwd = 4 * ML_WIDTH + ML_HEADS
    f_bwd = 4 * ML_WIDTH + 3 * ML_HEADS
    b_in = 0.02 * nrm(ks[5], (L, IN_COLS))
    b_in = b_in.at[:, f_fwd:f_fwd + ML_HEADS].add(fb).at[:, f_bwd:f_bwd + ML_HEADS].add(fb)
    return {
        'x': nrm(ks[0], (BATCH, SEQ, D_MODEL)),
        'mem': nrm(ks[1], (BATCH, MEM_LEN, D_MODEL)),
        'ln_in_w': 1.0 + 0.02 * nrm(ks[2], (D_MODEL,)),
        'ln_in_b': 0.02 * nrm(ks[3], (D_MODEL,)),
        'w_in': nrm(ks[4], (L, D_MODEL, IN_COLS)) * D_MODEL ** -0.5,
        'b_in': b_in,
        'ml_norm_w': 1.0 + 0.02 * nrm(ks[6], (L, ML_WIDTH)),
        'na_rpb': 0.02 * nrm(ks[7], (L, NA_HEADS, 2 * NA_ROWS - 1, 2 * NA_COLS - 1)),
        'w_mix_out': nrm(ks[8], (L, D_MODEL, D_MODEL)) * (D_MODEL ** -0.5 * DN_BETA),
        'b_mix_out': 0.02 * nrm(ks[9], (L, D_MODEL)),
        'ln1_w': 1.0 + 0.02 * nrm(ks[10], (L, D_MODEL)),
        'ln1_b': 0.02 * nrm(ks[11], (L, D_MODEL)),
        'w_xq': nrm(ks[12], (L, D_MODEL, D_MODEL)) * D_MODEL ** -0.5,
        'w_xkv': nrm(ks[13], (L, D_MODEL, 2 * D_MODEL)) * D_MODEL ** -0.5,
        'w_xo': nrm(ks[14], (L, D_MODEL, D_MODEL)) * (D_MODEL ** -0.5 * DN_BETA),
        'b_xo': 0.02 * nrm(ks[15], (L, D_MODEL)),
        'ln2_w': 1.0 + 0.02 * nrm(ks[16], (L, D_MODEL)),
        'ln2_b': 0.02 * nrm(ks[17], (L, D_MODEL)),
        'w_up': nrm(ks[18], (L, D_MODEL, 2 * D_FF)) * D_MODEL ** -0.5,
        'b_up': 0.02 * nrm(ks[19], (L, 2 * D_FF)),
        'w_dw': nrm(ks[20], (L, 3, 2 * D_FF)) * 3.0 ** -0.5,
        'b_dw': 0.02 * nrm(ks[21], (L, 2 * D_FF)),
        'w_down': nrm(ks[22], (L, D_FF, D_MODEL)) * (D_FF ** -0.5 * DN_BETA),
        'b_down': 0.02 * nrm(ks[23], (L, D_MODEL)),
        'ln3_w': 1.0 + 0.02 * nrm(ks[24], (L, D_MODEL)),
        'ln3_b': 0.02 * nrm(ks[25], (L, D_MODEL)),
    }


def reference(x, mem, ln_in_w, ln_in_b, w_in, b_in, ml_norm_w, na_rpb, w_mix_out, b_mix_out,
              ln1_w, ln1_b, w_xq, w_xkv, w_xo, b_xo, ln2_w, ln2_b,
              w_up, b_up, w_dw, b_dw, w_down, b_down, ln3_w, ln3_b):
    h = layer_norm(x, ln_in_w, ln_in_b)
    for l in range(DEPTH):
        h = layer_norm(DN_ALPHA * h + hybrid_mixer(h, w_in[l], b_in[l], ml_norm_w[l], na_rpb[l],
                                                    w_mix_out[l], b_mix_out[l]), ln1_w[l], ln1_b[l])
        h = layer_norm(DN_ALPHA * h + memory_xattn(h, mem, w_xq[l], w_xkv[l], w_xo[l], b_xo[l]),
                       ln2_w[l], ln2_b[l])
        h = layer_norm(DN_ALPHA * h + conv_ffn(h, w_up[l], b_up[l], w_dw[l], b_dw[l], w_down[l], b_down[l]),
                       ln3_w[l], ln3_b[l])
    return h
```

```python
import numpy as np
import concourse.bass as bass
import concourse.mybir as mybir
from concourse.bass_utils import run_bass_kernel_spmd

F32 = mybir.dt.float32
BF16 = mybir.dt.bfloat16
AF = mybir.ActivationFunctionType
ALU = mybir.AluOpType

ENGS = ("pe", "act", "dve", "pool", "sp")
DMA_K = 12

D = 1024
NT = 2048
NTL = 16
DEPTH = 4
DFF = 2816
ALPHA = (2.0 * DEPTH) ** 0.25
EPS = 1e-5
QS = 128.0 ** -0.5
NEG = -30000.0
SB_BASE = 16640
SB_END = 212992


class Buf:
    __slots__ = ("name", "w", "r", "excl")

    def __init__(self, name="", excl=False):
        self.name = name
        self.w = None
        self.r = []
        self.excl = excl


class Prog:
    def __init__(self, nc):
        self.nc = nc
        self.ops = {e: [] for e in ENGS}
        self.cnt = {e: 0 for e in ENGS}
        self.waited = {e: {} for e in ENGS}
        self.dma_cnt = {e: 0 for e in ENGS}
        self.fence = {}
        self.ncc = 0
        self.cc_vals = {}

    def barrier(self):
        f = {}
        for e in ENGS:
            if self.cnt[e] > 0:
                f[e] = self.cnt[e]
        for q in ENGS:
            n = self.dma_cnt[q]
            for j in range(min(n, DMA_K)):
                last_i = j + ((n - 1 - j) // DMA_K) * DMA_K
                f[("d", q, j)] = 16 * (last_i // DMA_K + 1)
        for key, v in self.cc_vals.items():
            f[key] = v
        self.fence = f

    def _deps(self, eng, reads, writes):
        need = dict(self.fence)

        def add(tok):
            k, v = tok
            if need.get(k, 0) < v:
                need[k] = v

        for b in reads:
            if b.excl:
                if b.w is not None and b.w[0] != eng:
                    add(b.w)
                continue
            if b.w is not None and not (eng == "pe" and b.w[0] == "pe"):
                add(b.w)
        for b in writes:
            if b.excl:
                if b.w is not None and b.w[0] != eng:
                    add(b.w)
                continue
            if b.w is not None and not (eng == "pe" and b.w[0] == "pe"):
                add(b.w)
            for t in b.r:
                if t[0] != eng:
                    add(t)
        waits = []
        wd = self.waited[eng]
        for k, v in need.items():
            if wd.get(k, 0) < v:
                wd[k] = v
                waits.append((k, v))
        return waits

    def _commit(self, tok, reads, writes):
        for b in reads:
            if b.excl:
                b.w = tok
            else:
                b.r.append(tok)
        for b in writes:
            b.w = tok
            b.r = []

    def op(self, eng, fn, reads=(), writes=()):
        waits = self._deps(eng, reads, writes)
        self.cnt[eng] += 1
        tok = (eng, self.cnt[eng])
        self.ops[eng].append((waits, fn, (eng, 1)))
        self._commit(tok, reads, writes)

    def dma(self, q, out, in_, reads=(), writes=()):
        waits = self._deps(q, reads, writes)
        i = self.dma_cnt[q]
        self.dma_cnt[q] += 1
        j = i % DMA_K
        val = 16 * (i // DMA_K + 1)
        key = ("d", q, j)
        if val > 16:
            wd = self.waited[q]
            if wd.get(key, 0) < val - 16:
                wd[key] = val - 16
                waits.append((key, val - 16))
        self.ops[q].append((waits, lambda e: e.dma_start(out=out, in_=in_), (key, 16)))
        self._commit((key, val), reads, writes)

    def dma_fn(self, q, fn, reads=(), writes=()):
        waits = self._deps(q, reads, writes)
        i = self.dma_cnt[q]
        self.dma_cnt[q] += 1
        j = i % DMA_K
        val = 16 * (i // DMA_K + 1)
        key = ("d", q, j)
        if val > 16:
            wd = self.waited[q]
            if wd.get(key, 0) < val - 16:
                wd[key] = val - 16
                waits.append((key, val - 16))
        self.ops[q].append((waits, fn, (key, 16)))
        self._commit((key, val), reads, writes)

    def cc(self, fn, reads=(), writes=(), inc=1):
        waits = self._deps("pool", reads, writes)
        self.ncc += 1
        key = ("cc", self.ncc)
        self.cc_vals[key] = inc
        self.ops["pool"].append((waits, fn, (key, inc)))
        self._commit((key, inc), reads, writes)

    def finish(self):
        wd = self.waited["sp"]
        waits = []
        for q in ENGS:
            n = self.dma_cnt[q]
            for j in range(min(n, DMA_K)):
                last_i = j + ((n - 1 - j) // DMA_K) * DMA_K
                val = 16 * (last_i // DMA_K + 1)
                key = ("d", q, j)
                if wd.get(key, 0) < val:
                    wd[key] = val
                    waits.append((key, val))
        for e in ENGS:
            if e != "sp" and self.cnt[e] > 0:
                waits.append((e, self.cnt[e]))
        self.ops["sp"].append((waits, None, None))

    def emit(self):
        import contextlib
        nc = self.nc
        with contextlib.ExitStack() as st:
            sems = {}
            for e in ENGS:
                sems[e] = st.enter_context(nc.semaphore("s_" + e))
            for q in ENGS:
                for j in range(min(self.dma_cnt[q], DMA_K)):
                    sems[("d", q, j)] = st.enter_context(nc.semaphore("d_%s_%d" % (q, j)))
            for j in range(getattr(self, "ncc", 0)):
                sems[("cc", j + 1)] = st.enter_context(nc.semaphore("cc_%d" % j))
            block = st.enter_context(nc.Block())

            def run(ename):
                def body(eng):
                    for waits, fn, inc in self.ops[ename]:
                        for k, v in waits:
                            eng.wait_ge(sems[k], v)
                        if fn is not None:
                            fn(eng).then_inc(sems[inc[0]], inc[1])
                return body

            block.tensor(run("pe"))
            block.scalar(run("act"))
            block.vector(run("dve"))
            block.gpsimd(run("pool"))
            block.sync(run("sp"))


class Rot:
    def __init__(self, k, name, shape, dt, n):
        self.t = [k.alloc(shape, dt) for i in range(n)]
        self.b = [Buf("%s%d" % (name, i)) for i in range(n)]
        self.i = 0

    def next(self):
        k = self.i % len(self.t)
        self.i += 1
        return self.t[k], self.b[k]


class K:
    def __init__(self, nc):
        self.nc = nc
        self.P = Prog(nc)
        self.psf = [nc.alloc_psum_tensor("psf%d" % i, [128, 512], F32) for i in range(6)]
        self.psfB = [Buf("psf%d" % i, excl=True) for i in range(6)]
        self.psb = [nc.alloc_psum_tensor("psb%d" % i, [128, 8, 128], BF16) for i in range(2)]
        self.psbB = [Buf("psb%d" % i, excl=True) for i in range(2)]
        self.pi = 0
        self.pbi = 0
        self.nrot = 5
        self.ln_eng = "pool"
        self.din = {}
        self.dout = {}
        self.sp = SB_BASE
        self.stack = []
        self.nalloc = 0
        self.sp_max = SB_BASE

    def alloc(self, shape, dt):
        n = 1
        for d_ in shape[1:]:
            n *= d_
        nbytes = n * (2 if dt == BF16 else 4)
        nbytes = (nbytes + 63) // 64 * 64
        assert self.sp + nbytes <= SB_END, "SBUF overflow: need %d" % (self.sp + nbytes - SB_END)
        self.nalloc += 1
        t = self.nc.alloc_sbuf_tensor_at("sbt%d" % self.nalloc, list(shape), dt, offset=self.sp)
        self.sp += nbytes
        self.sp_max = max(self.sp_max, self.sp)
        return t

    def push(self):
        self.stack.append(self.sp)

    def pop(self):
        self.P.barrier()
        self.sp = self.stack.pop()

    def scratch(self, name, shape):
        return self.nc.dram_tensor(name, list(shape), F32).ap()

    def inp(self, name, shape):
        t = self.nc.dram_tensor(name, list(shape), F32, kind="ExternalInput").ap()
        self.din[name] = t
        return t

    def out(self, name, shape):
        t = self.nc.dram_tensor(name, list(shape), F32, kind="ExternalOutput").ap()
        self.dout[name] = t
        return t

    def sb(self, name, shape, dt=F32):
        return self.alloc(list(shape), dt), Buf(name)

    def ps(self):
        k = self.pi % self.nrot
        self.pi += 1
        return self.psf[k], self.psfB[k]

    def psx(self):
        return self.psf[5], self.psfB[5]

    def packed(self, tag, banks, nreg):
        key = (tag, tuple(banks), nreg)
        if not hasattr(self, "_packed"):
            self._packed = {}
        if key not in self._packed:
            self._packed[key] = [[(b, [Buf("pk%s_%d_%d" % (tag, b, r)) for r in range(nreg)]) for b in banks], 0]
        ent = self._packed[key]
        b, bufs = ent[0][ent[1] % len(ent[0])]
        ent[1] += 1
        return self.psf[b], bufs, self.psfB[b]

    def pb(self):
        k = self.pbi % 2
        self.pbi += 1
        return self.psb[k], self.psbB[k]

    def mm(self, out, lhsT, rhs, start, stop, r, w):
        self.P.op("pe", lambda e: e.matmul(out, lhsT, rhs, start=start, stop=stop), r, w)

    def tr(self, out, in_, r, w):
        ident = self.ident
        self.P.op("pe", lambda e: e.transpose(out, in_, ident[:]), list(r) + [self.identB], w)

    def act(self, out, in_, func, r, w, bias=None, scale=None):
        kw = {}
        if bias is not None:
            kw["bias"] = bias
        if scale is not None:
            kw["scale"] = scale
        self.P.op("act", lambda e: e.activation(out, in_, func, **kw), r, w)

    def tt(self, eng, out, a, b, op, r, w):
        self.P.op(eng, lambda e: e.tensor_tensor(out, a, b, op), r, w)

    def ts(self, eng, out, in0, s1, s2, op0, op1, r, w):
        if s2 is None:
            self.P.op(eng, lambda e: e.tensor_scalar(out, in0, s1, None, op0), r, w)
        else:
            self.P.op(eng, lambda e: e.tensor_scalar(out, in0, s1, s2, op0, op1), r, w)

    def stt(self, out, in0, scalar, in1, op0, op1, r, w):
        self.P.op("dve", lambda e: e.scalar_tensor_tensor(out, in0, scalar, in1, op0, op1), r, w)

    def cp(self, eng, out, in_, r, w):
        if eng == "act":
            self.P.op("act", lambda e: e.copy(out, in_), r, w)
        else:
            self.P.op(eng, lambda e: e.tensor_copy(out, in_), r, w)

    def recip(self, out, in_, r, w):
        self.P.op("dve", lambda e: e.reciprocal(out, in_), r, w)

    def memset(self, eng, ap, val, w):
        self.P.op(eng, lambda e: e.memset(ap, val), (), w)

    def dma(self, q, out, in_, r=(), w=()):
        self.P.dma(q, out, in_, r, w)

    def load_consts(self):
        c = self.inp("consts", [128, 4, 128])
        self.ident, self.identB = self.sb("ident", [128, 128], BF16)
        self.cf, self.cfB = self.sb("cf", [128, 4, 128], F32)
        self.dma("pool", self.ident[:], c[:, 0, :], w=[self.identB])
        self.dma("sp", self.cf[:], c[:, :, :], w=[self.cfB])

    def bc_load(self, name, row_ap, n):
        t, b = self.sb(name, [128, n], F32)
        self.dma("sp", t[:], row_ap.partition_broadcast(128), w=[b])
        return t, b

    def ln_setup(self, n=2):
        k = self
        self.ln_st = Rot(k, "lnst", [128, 2, 6], F32, n)
        self.ln_mv = Rot(k, "lnmv", [128, 2], F32, n)
        self.ln_rs = Rot(k, "lnrs", [128, 1], F32, n)

    def ln_stage1(self, r, rB):
        st, stB = self.ln_st.next()
        mv, mvB = self.ln_mv.next()
        rs, rsB = self.ln_rs.next()
        P = self.P
        for j in range(2):
            P.op("dve", lambda e, j=j: e.bn_stats(st[:, j, :], r[:, j * 512:(j + 1) * 512]), [rB], [stB])
        P.op("dve", lambda e: e.bn_aggr(mv[:], st[:]), [stB], [mvB])
        self.act(rs[:], mv[:, 1:2], AF.Sqrt, [mvB], [rsB], bias=EPS, scale=1.0)
        return mv, mvB, rs, rsB

    def ln_stage2(self, r, rB, mv, mvB, rs, rsB, wbc, wB, bbc, bB):
        self.recip(rs[:], rs[:], [rsB], [rsB])
        self.stt(r[:], r[:], mv[:, 0:1], wbc[:], ALU.subtract, ALU.mult, [rB, mvB, wB], [rB])
        self.stt(r[:], r[:], rs[:, 0:1], bbc[:], ALU.mult, ALU.add, [rB, rsB, bB], [rB])

    def ln_tile(self, r, rB, wbc, wB, bbc, bB):
        mv, mvB, rs, rsB = self.ln_stage1(r, rB)
        self.ln_stage2(r, rB, mv, mvB, rs, rsB, wbc, wB, bbc, bB)

    def to_hT(self, r, rB, hT, hTB, col0):
        hb, hbB = self.hb_rot.next()
        self.cp("act", hb[:], r[:], [rB], [hbB])
        self.hb_to_hT(hb, hbB, hT, hTB, col0)

    def hb_to_hT(self, hb, hbB, hT, hTB, col0):
        pb, pbB = self.pb()
        for kc in range(8):
            self.tr(pb[:, kc, :], hb[:, kc * 128:(kc + 1) * 128], [hbB], [pbB])
        self.cp("act", hT[:, :, col0:col0 + 128], pb[:, :, :], [pbB], [hTB])

    def load_hT(self, h_dram, ntiles, hT, hTBs, tile0=0):
        for i in range(ntiles):
            hb, hbB = self.hb_rot.next()
            self.dma("pool", hb[:], h_dram[i * 128:(i + 1) * 128, :], w=[hbB])
            self.hb_to_hT(hb, hbB, hT, hTBs[tile0 + i], (tile0 + i) * 128)

    def load_w(self, name, w_dram, c0, c1, kchunks=8, q="pool"):
        n = c1 - c0
        t = self.alloc([128, kchunks, n], BF16)
        bs = [Buf("%s_%d" % (name, k)) for k in range(kchunks)]
        for kc in range(kchunks):
            self.dma(q, t[:, kc, :], w_dram[kc * 128:(kc + 1) * 128, c0:c1], w=[bs[kc]])
        return t, bs

    def tail_fill(self, r, rB, cg, p, pB, bias_bc, biasB):
        self.tt("dve", r[:, cg * 512:(cg + 1) * 512], p[:, :], bias_bc[:, cg * 512:(cg + 1) * 512], ALU.add, [pB, biasB], [rB])

    def tail_finish(self, r, rB, hsrc, ti, lnw, lnwB, lnb, lnbB, hdst, hT=None, hTB=None):
        hh, hhB = self.h_rot.next()
        self.dma("sp", hh[:], hsrc[ti * 128:(ti + 1) * 128, :], w=[hhB])
        self.stt(r[:], hh[:], ALPHA, r[:], ALU.mult, ALU.add, [hhB, rB], [rB])
        mv, mvB, rs, rsB = self.ln_stage1(r, rB)
        self.tail_flush()

        def stage2():
            self.ln_stage2(r, rB, mv, mvB, rs, rsB, lnw, lnwB, lnb, lnbB)
            self.dma("sp", hdst[ti * 128:(ti + 1) * 128, :], r[:], r=[rB])
            if hT is not None:
                self.to_hT(r, rB, hT, hTB, ti * 128)

        self._deferred = stage2

    def tail_flush(self):
        d = getattr(self, "_deferred", None)
        self._deferred = None
        if d is not None:
            d()

    def res_ln_tail(self, pss, bias_bc, biasB, hsrc, ti, lnw, lnwB, lnb, lnbB, hdst, hT=None, hTB=None):
        r, rB = self.r_rot.next()
        for cg in range(2):
            self.tail_fill(r, rB, cg, pss[cg][0], pss[cg][1], bias_bc, biasB)
        self.tail_finish(r, rB, hsrc, ti, lnw, lnwB, lnb, lnbB, hdst, hT, hTB)

    def tail_setup(self, nr=2, nh=2):
        k = self
        self.ln_setup(max(nr, 2))
        self.r_rot = Rot(k, "rres", [128, 1024], F32, nr)
        self.h_rot = Rot(k, "hres", [128, 1024], F32, nh)

    def finish(self):
        self.P.finish()
        self.P.emit()


class IO:
    def __init__(self, k, m=None):
        self.k, self.m = k, m

    def i(self, name, shape):
        return self.m[name] if self.m is not None else self.k.inp(name, shape)

    def o(self, name, shape):
        return self.m[name] if self.m is not None else self.k.out(name, shape)


def build_pre(k=None, io=None):
    solo = k is None
    if solo:
        nc = bass.Bass("TRN2", target_bir_lowering=False)
        k = K(nc)
    io = IO(k, io)
    x = io.i("x", [NT, D])
    lw = io.i("lnw", [1, D])
    lb = io.i("lnb", [1, D])
    h = io.o("h", [NT, D])
    wbc, wB = k.bc_load("wbc", lw[0:1, :], D)
    bbc, bB = k.bc_load("bbc", lb[0:1, :], D)
    k.ln_setup(4)
    rr = Rot(k, "r", [128, 1024], F32, 6)
    for i in range(NTL):
        r, rB = rr.next()
        k.dma("sp", r[:], x[i * 128:(i + 1) * 128, :], w=[rB])
        k.ln_tile(r, rB, wbc, wB, bbc, bB)
        k.dma("sp", h[i * 128:(i + 1) * 128, :], r[:], r=[rB])
    if solo:
        k.finish()
        return nc, k


def ml_inproj(k, h_dram, w_main, w_gate, bfm, bfmB, brow, need_mo, ext_hT=None):
    nc = k.nc
    qT, qTB = k.sb("qT", [128, 4, NT], BF16)
    kT, kTB = k.sb("kT", [128, 4, NT], BF16)
    ktok, ktokB = k.sb("ktok", [128, NTL, 512], BF16)
    vaug, vaugB = k.sb("vaug", [128, NTL, 4, 130], BF16)
    gates, gatesB = k.sb("gates", [128, NTL, 16], F32)
    mo = moB = None
    if need_mo:
        mo, moB = k.sb("mo", [128, NTL, 512], BF16)
    k.push()
    if ext_hT is not None:
        hT, hTB = ext_hT
    else:
        k.hb_rot = Rot(k, "hb", [128, 1024], BF16, 3)
        hT, _ = k.sb("hT", [128, 8, NT], BF16)
        hTB = [Buf("hT%d" % i) for i in range(NTL)]
        k.load_hT(h_dram, NTL, hT, hTB)
    wml, wmlB = k.load_w("wml", w_main, 0, 2048)
    wg, wgB = k.load_w("wg", w_gate, 0, 16)
    bq, bqB = k.sb("bq_s", [128, 4], F32)
    k.ts("pool", bq[:], bfm[:, 0:4], QS, None, ALU.mult, None, [bfmB], [bqB])
    bck, bckB = k.bc_load("bc_k", brow[0:1, 512:1024], 512)
    bcv, bcvB = k.bc_load("bc_v", brow[0:1, 1024:1536], 512)
    bcg, bcgB = k.bc_load("bc_g", brow[0:1, 3584:3600], 16)
    qTBs = [Buf("qT%d" % b) for b in range(4)]
    kTBs = [Buf("kT%d" % b) for b in range(4)]
    tokBs = [Buf("tok%d" % i) for i in range(NTL)]
    k.memset("pool", vaug[:, :, :, 128:130], 1.0, tokBs)
    if need_mo:
        bcm, bcmB = k.bc_load("bc_mo", brow[0:1, 1536:2048], 512)
    for blk in range(4):
        cs = slice(blk * 512, (blk + 1) * 512)
        hr = hTB[blk * 4:(blk + 1) * 4]
        for c in range(8):
            p, pB = k.ps()
            for kc in range(8):
                k.mm(p[:, :], wml[:, kc, c * 128:(c + 1) * 128], hT[:, kc, cs], kc == 0, kc == 7, [wmlB[kc]] + hr, [pB])
            if c < 4:
                k.act(qT[:, c, cs], p[:, :], AF.Identity, [pB, bqB], [qTBs[blk]], bias=bq[:, c:c + 1], scale=QS)
            else:
                k.act(kT[:, c - 4, cs], p[:, :], AF.Identity, [pB, bfmB], [kTBs[blk]], bias=bfm[:, c:c + 1], scale=1.0)
        for st in range(4):
            ti = blk * 4 + st
            ts_ = slice(ti * 128, (ti + 1) * 128)
            p, pB = k.ps()
            for kc in range(8):
                k.mm(p[:, :], hT[:, kc, ts_], wml[:, kc, 512:1024], kc == 0, kc == 7, [wmlB[kc], hTB[ti]], [pB])
            k.tt("dve", ktok[:, ti, :], p[:, :], bck[:], ALU.add, [pB, bckB], [tokBs[ti]])
            p, pB = k.ps()
            for kc in range(8):
                k.mm(p[:, :], hT[:, kc, ts_], wml[:, kc, 1024:1536], kc == 0, kc == 7, [wmlB[kc], hTB[ti]], [pB])
            for hd in range(4):
                k.tt("dve", vaug[:, ti, hd, 0:128], p[:, hd * 128:(hd + 1) * 128], bcv[:, hd * 128:(hd + 1) * 128], ALU.add,
                     [pB, bcvB], [tokBs[ti]])
            if need_mo:
                p, pB = k.ps()
                for kc in range(8):
                    k.mm(p[:, :], hT[:, kc, ts_], wml[:, kc, 1536:2048], kc == 0, kc == 7, [wmlB[kc], hTB[ti]], [pB])
                k.tt("dve", mo[:, ti, :], p[:, :], bcm[:], ALU.add, [pB, bcmB], [tokBs[ti]])
            p, pB = k.ps()
            for kc in range(8):
                k.mm(p[:, 0:16], hT[:, kc, ts_], wg[:, kc, :], kc == 0, kc == 7, [wgB[kc], hTB[ti]], [pB])
            k.tt("dve", gates[:, ti, :], p[:, 0:16], bcg[:], ALU.add, [pB, bcgB], [tokBs[ti]])
    k.pop()
    return dict(qT=qT, kT=kT, ktok=ktok, vaug=vaug, gates=gates, mo=mo, qTBs=qTBs, kTBs=kTBs, tokBs=tokBs)


def ml_scan(k, m, sset, cinit, emit_out, emit_state):
    nc = k.nc
    qT, kT, ktok, vaug, gates = m["qT"], m["kT"], m["ktok"], m["vaug"], m["gates"]
    tokBs = m["tokBs"]
    gi = gates[:, :, 8 * sset:8 * sset + 4]
    gf = gates[:, :, 8 * sset + 4:8 * sset + 8]
    tri = k.cf[:, 1 + sset, :]
    sfx = str(sset)
    nlf, nlfB = k.sb("nlf" + sfx, [128, NTL, 4], F32)
    tmp, tmpB = k.sb("gtmp" + sfx, [128, NTL, 4], F32)
    ee, eeB = k.sb("ee" + sfx, [128, NTL, 4], F32)
    ena, enaB = k.sb("ena" + sfx, [128, NTL, 4], F32)
    lam, lamB = k.sb("lam" + sfx, [128, NTL, 4], F32)
    k.act(tmp[:], gf, AF.Exp, tokBs, [tmpB], scale=-1.0)
    k.act(nlf[:], tmp[:], AF.Ln, [tmpB], [nlfB], bias=1.0)
    pA, pAB = k.psf[0], k.psfB[0]
    nlf2 = nlf[:].rearrange("p a b -> p (a b)")
    k.mm(pA[:, 0:64], tri, nlf2, True, True, [k.cfB, nlfB], [pAB])
    pE, pEB = pA, pAB
    k.mm(pE[:, 64:128], k.cf[:, 3, :], nlf2, True, True, [k.cfB, nlfB], [pEB])
    pA3 = pA[:, 0:64].rearrange("p (a b) -> p a b", b=4)
    pE3 = pE[:, 64:128].rearrange("p (a b) -> p a b", b=4)
    k.tt("dve", tmp[:], gi, pA3, ALU.add, tokBs + [pAB], [tmpB])
    k.act(ee[:], tmp[:], AF.Exp, [tmpB], [eeB])
    k.act(ena[:], pA3, AF.Exp, [pAB], [enaB])
    k.act(lam[:], pE3, AF.Exp, [pEB], [lamB], scale=-1.0)
    gB = [nlfB, eeB, enaB, lamB]

    Dst = [k.sb("D%s_%d" % (sfx, hd), [128, 129], F32) for hd in range(4)]
    Cbf = [Rot(k, "Cbf%s_%d" % (sfx, hd), [128, 130], BF16, 2) for hd in range(4)]
    ev_rot = Rot(k, "ev" + sfx, [128, 130], BF16, 6)
    wt_rot = Rot(k, "wt" + sfx, [128, 128], BF16, 6)
    t1_rot = Rot(k, "t1" + sfx, [128, 4], F32, 4)
    order = list(range(NTL)) if sset == 0 else list(range(NTL - 1, -1, -1))
    cur = [None] * 4
    if cinit is not None:
        ci, ciB = cinit
        for hd in range(4):
            cb, cbB = Cbf[hd].next()
            k.cp("act", cb[:, 0:129], ci[:, hd, :], [ciB], [cbB])
            cur[hd] = (cb, cbB)
    items = [(n, c, hd) for n, c in enumerate(order) for hd in range(4)]
    ctx = {}

    def stage_p(it):
        n, c, hd = it
        cs = slice(c * 128, (c + 1) * 128)
        blk = c // 4
        ev, evB = ev_rot.next()
        k.act(ev[:, 0:129], vaug[:, c, hd, 0:129], AF.Identity, [tokBs[c], eeB], [evB], scale=ee[:, c, hd:hd + 1])
        pk, pkB, bkB = k.packed("scan", (1, 2, 3, 4, 5), 3)
        k.mm(pk[:, 0:128], kT[:, hd, cs], qT[:, hd, cs], True, True, [m["kTBs"][blk], m["qTBs"][blk]], [pkB[0], bkB])
        wt, wtB = wt_rot.next()
        k.tt("dve", wt[:], pk[:, 0:128], tri, ALU.mult, [pkB[0], bkB, k.cfB], [wtB])
        ctx[it] = (ev, evB, pk, pkB, wt, wtB, bkB)

    def stage_m(it):
        n, c, hd = it
        cs = slice(c * 128, (c + 1) * 128)
        blk = c // 4
        ev, evB, pk, pkB, wt, wtB, bkB = ctx[it]
        prev = order[n - 1] if n > 0 else None
        pO, pOB = pk[:, 128:257], pkB[1]
        has_c = cur[hd] is not None
        k.mm(pO[:, 0:129], wt[:], ev[:, 0:129], True, not has_c, [wtB, evB], [pOB, bkB])
        if has_c:
            cb, cbB = cur[hd]
            k.mm(pO[:, 0:129], qT[:, hd, cs], cb[:, 0:129], False, True, [m["qTBs"][blk], cbB], [pOB, bkB])
        pG, pGB = pk[:, 257:386], pkB[2]
        k.mm(pG[:, 0:129], ktok[:, c, hd * 128:(hd + 1) * 128], ev[:, 0:129], True, True, [tokBs[c], evB], [pGB, bkB])
        Dt, DB = Dst[hd]
        if n == 0:
            if cinit is None:
                k.cp("dve", Dt[:], pG[:, 0:129], [pGB, bkB], [DB])
            else:
                k.tt("dve", Dt[:], pG[:, 0:129], cinit[0][:, hd, :], ALU.add, [pGB, bkB, cinit[1]], [DB])
        else:
            k.stt(Dt[:], Dt[:], lam[:, prev, hd:hd + 1], pG[:, 0:129], ALU.mult, ALU.add, [DB, lamB, pGB, bkB], [DB])
        if n < NTL - 1:
            cb, cbB = Cbf[hd].next()
            k.act(cb[:, 0:129], Dt[:], AF.Identity, [DB, lamB], [cbB], scale=lam[:, c, hd:hd + 1])
            cur[hd] = (cb, cbB)
        if hd == 0:
            ctx["t1"] = t1_rot.next()
        t1, t1B = ctx["t1"]
        k.act(t1[:, hd:hd + 1], pO[:, 128:129], AF.Abs, [pOB, bkB], [t1B])
        ctx[it] = (pO, (pOB, bkB), t1, t1B)

    def stage_e(n, c):
        t1, t1B = ctx[(n, c, 0)][2:4]
        k.tt("dve", t1[:], t1[:], ena[:, c, :], ALU.max, [t1B, enaB], [t1B])
        k.recip(t1[:], t1[:], [t1B], [t1B])
        for hd in range(4):
            pO, pOB, _, _ = ctx.pop((n, c, hd))
            emit_out(c, hd, pO, pOB, t1[:, hd:hd + 1], t1B)

    stage_p(items[0])
    for i, it in enumerate(items):
        if i + 1 < len(items):
            stage_p(items[i + 1])
        stage_m(it)
        if it[2] == 3:
            stage_e(it[0], it[1])
    prev = order[-1]
    k.nrot = 5
    if emit_state is not None:
        for hd in range(4):
            emit_state(hd, Dst[hd][0], Dst[hd][1], lam[:, prev, hd:hd + 1], lamB)


NA_KTS = {0: [0, 1, 2, 3], 1: [0, 1, 2, 3], 14: [12, 13, 14, 15, 16], 15: [13, 14, 15, 16, 17]}
NA_OFF = {0: 5, 1: 9, 14: 13, 15: 18}


def build_a1(k=None, io=None):
    solo = k is None
    if solo:
        nc = bass.Bass("TRN2", target_bir_lowering=False)
        k = K(nc)
    io = IO(k, io)
    h = io.i("h", [NT, D])
    hh = io.i("hh", [256, D])
    w_main = io.i("w_main", [D, 3584])
    bfm_d = io.i("bfm", [128, 28])
    brow = io.i("brow", [1, 3600])
    rpbt = io.i("rpbt", [8, 128, 23, 128])
    y_na = io.o("y_na", [NT, 512])
    if solo:
        k.load_consts()
    k.hb_rot = Rot(k, "hb", [128, 1024], BF16, 3)
    if io.m is not None and "hT" in io.m:
        hT, hTB = io.m["hT"]
    else:
        hT, _ = k.sb("hT", [128, 8, NT + 256], BF16)
        hTB = [Buf("hT%d" % i) for i in range(18)]
    bfm, bfmB = k.sb("bfm", [128, 28], F32)
    k.dma("sp", bfm[:], bfm_d[:, :], w=[bfmB])
    k.load_hT(h, NTL, hT, hTB)
    k.load_hT(hh, 2, hT, hTB, tile0=16)

    wna, wnaB = k.load_w("wna", w_main, 2048, 3584)
    bnq, bnqB = k.sb("bnq", [128, 4], F32)
    k.ts("pool", bnq[:], bfm[:, 16:20], 0.125, None, ALU.mult, None, [bfmB], [bnqB])
    bcnv, bcnvB = k.bc_load("bc_nv", brow[0:1, 3072:3584], 512)
    nqT, _ = k.sb("nqT", [128, 4, NT], BF16)
    nkT, _ = k.sb("nkT", [128, 4, NT + 256], BF16)
    nva, _ = k.sb("nva", [128, 18, 8, 66], BF16)
    nqB = [Buf("nq%d" % b) for b in range(4)]
    nkB = [Buf("nk%d" % b) for b in range(5)]
    nvB = [Buf("nv%d" % i) for i in range(18)]
    k.memset("pool", nva[:, :, :, 64:66], 1.0, nvB)
    for blk in range(5):
        n = 512 if blk < 4 else 256
        cs = slice(blk * 512, blk * 512 + n)
        hr = hTB[blk * 4:blk * 4 + n // 128]
        for hp in range(4):
            if blk < 4:
                p, pB = k.ps()
                for kc in range(8):
                    k.mm(p[:, 0:n], wna[:, kc, hp * 128:(hp + 1) * 128], hT[:, kc, cs], kc == 0, kc == 7, [wnaB[kc]] + hr, [pB])
                k.act(nqT[:, hp, cs], p[:, 0:n], AF.Identity, [pB, bnqB], [nqB[blk]], bias=bnq[:, hp:hp + 1], scale=0.125)
            p, pB = k.ps()
            for kc in range(8):
                k.mm(p[:, 0:n], wna[:, kc, 512 + hp * 128:512 + (hp + 1) * 128], hT[:, kc, cs], kc == 0, kc == 7, [wnaB[kc]] + hr, [pB])
            k.act(nkT[:, hp, cs], p[:, 0:n], AF.Identity, [pB, bfmB], [nkB[blk]], bias=bfm[:, 20 + hp:21 + hp], scale=1.0)
    for ti in range(18):
        p, pB = k.ps()
        for kc in range(8):
            k.mm(p[:, :], hT[:, kc, ti * 128:(ti + 1) * 128], wna[:, kc, 1024:1536], kc == 0, kc == 7, [wnaB[kc], hTB[ti]], [pB])
        k.tt("dve", nva[:, ti, :, 0:64], p[:, :].rearrange("p (a b) -> p a b", b=64),
             bcnv[:].rearrange("p (a b) -> p a b", b=64), ALU.add, [pB, bcnvB], [nvB[ti]])

    k.P.barrier()
    k.nrot = 4
    ytile = Rot(k, "yna", [128, 512], F32, 2)
    bt_rot = Rot(k, "btile", [128, 23, 128], BF16, 2)
    pT_rot = Rot(k, "pT", [128, 5, 128], BF16, 4)
    rd_rot = Rot(k, "rd", [128, 1], F32, 8)
    ysb, ysbB = k.sb("ysb", [128, NTL, 512], F32)
    yBs = [Buf("y%d" % i) for i in range(NTL)]
    naslot = [0]
    naslotB = [Buf("naslot%d" % i) for i in range(14)]
    na_items = [(hd, qp) for hd in range(8) for qp in range(NTL)]
    nactx = {}
    btcur = {}

    def na_s(it):
        hd, qp = it
        if qp == 0:
            bt, btB = bt_rot.next()
            k.dma("pool", bt[:], rpbt[hd, :, :, :], w=[btB])
            btcur[hd] = (bt, btB)
        bt, btB = btcur[hd]
        hp, r0 = hd // 2, 64 * (hd % 2)
        kts = NA_KTS.get(qp, [qp - 2, qp - 1, qp, qp + 1, qp + 2])
        off = NA_OFF.get(qp, 0)
        qs = slice(qp * 128, (qp + 1) * 128)
        p1, p1B = k.ps()
        p2, p2B = k.ps()
        for i, kt in enumerate(kts):
            pp, ppB = (p1, p1B) if i < 4 else (p2, p2B)
            o = (i % 4) * 128
            kblk = min(kt // 4, 4)
            k.mm(pp[:, o:o + 128], nkT[r0:r0 + 64, hp, kt * 128:(kt + 1) * 128], nqT[r0:r0 + 64, hp, qs], True, False,
                 [nkB[kblk], nqB[qp // 4]], [ppB])
            k.mm(pp[:, o:o + 128], k.ident[:], bt[:, off + i, :], False, True, [k.identB, btB], [ppB])
        pT, pTB = pT_rot.next()
        n1 = min(len(kts), 4)
        k.act(pT[:, 0:n1, :], p1[:, 0:n1 * 128].rearrange("p (a b) -> p a b", b=128), AF.Exp, [p1B], [pTB])
        if len(kts) == 5:
            k.act(pT[:, 4, :], p2[:, 0:128], AF.Exp, [p2B], [pTB])
        nactx[it] = (kts, pT, pTB)

    def na_v(it):
        hd, qp = it
        kts, pT, pTB = nactx.pop(it)
        slot = naslot[0] % 14
        naslot[0] += 1
        bank = 4 + slot % 2
        pO = k.psf[bank][:, (slot // 2) * 66:(slot // 2) * 66 + 66]
        pOB = naslotB[slot]
        bkB = k.psfB[bank]
        for i, kt in enumerate(kts):
            k.mm(pO[:, 0:65], pT[:, i, :], nva[:, kt, hd, 0:65], i == 0, i == len(kts) - 1, [pTB, nvB[kt]], [pOB, bkB])
        rd, rdB = rd_rot.next()
        k.recip(rd[:], pO[:, 64:65], [pOB, bkB], [rdB])
        k.ts("dve", ysb[:, qp, hd * 64:(hd + 1) * 64], pO[:, 0:64], rd[:, 0:1], None, ALU.mult, None, [pOB, bkB, rdB], [yBs[qp]])

    na_s(na_items[0])
    for i, it in enumerate(na_items):
        if i + 1 < len(na_items):
            na_s(na_items[i + 1])
        na_v(it)
    k.nrot = 5
    for qp in range(NTL):
        k.dma("sp", y_na[qp * 128:(qp + 1) * 128, :], ysb[:, qp, :], r=[yBs[qp]])

    if solo:
        k.finish()
        return nc, k


def build_a2(k=None, io=None):
    solo = k is None
    if solo:
        nc = bass.Bass("TRN2", target_bir_lowering=False)
        k = K(nc)
    io = IO(k, io)
    h = io.i("h", [NT, D])
    w_main = io.i("w_main", [D, 3584])
    w_gate = io.i("w_gate", [D, 16])
    bfm_d = io.i("bfm", [128, 28])
    brow = io.i("brow", [1, 3600])
    hd1_o = io.o("hd1", [NT, 512])
    st_o = io.o("st1", [128, 4, 129])
    mo_o = io.o("mo", [NT, 512])
    if solo:
        k.load_consts()
    bfm, bfmB = k.sb("bfm", [128, 28], F32)
    k.dma("sp", bfm[:], bfm_d[:, :], w=[bfmB])
    m = ml_inproj(k, h, w_main, w_gate, bfm, bfmB, brow, need_mo=True)
    hd_rot = Rot(k, "hdt", [128, 512], F32, 3)
    state = {}

    def emit_out(c, hd, pO, pOB, t1, t1B):
        if hd == 0:
            state["t"] = hd_rot.next()
        t, tB = state["t"]
        k.ts("dve", t[:, hd * 128:(hd + 1) * 128], pO[:, 0:128], t1[:, 0:1], None, ALU.mult, None, list(pOB) + [t1B], [tB])
        if hd == 3:
            k.dma("sp", hd1_o[c * 128:(c + 1) * 128, :], t[:], r=[tB])

    sto, stoB = k.sb("sto", [128, 4, 129], F32)

    def emit_state(hd, Dt, DB, lam_ap, lamB):
        k.act(sto[:, hd, :], Dt[:], AF.Identity, [DB, lamB], [stoB], scale=lam_ap)
        if hd == 3:
            k.dma("sp", st_o[:, :, :], sto[:], r=[stoB])

    ml_scan(k, m, 0, None, emit_out, emit_state)
    for ti in range(NTL):
        k.dma("pool", mo_o[ti * 128:(ti + 1) * 128, :], m["mo"][:, ti, :], r=[m["tokBs"][ti]])
    if solo:
        k.finish()
        return nc, k


def build_b1(k=None, io=None):
    solo = k is None
    if solo:
        nc = bass.Bass("TRN2", target_bir_lowering=False)
        k = K(nc)
    io = IO(k, io)
    h = io.i("h", [NT, D])
    w_main = io.i("w_main", [D, 3584])
    w_gate = io.i("w_gate", [D, 16])
    bfm_d = io.i("bfm", [128, 28])
    brow = io.i("brow", [1, 3600])
    hd1 = io.i("hd1", [NT, 512])
    st_in = io.i("st_in", [128, 4, 129])
    hm_o = io.o("hm", [NT, 512])
    if solo:
        k.load_consts()
    bfm, bfmB = k.sb("bfm", [128, 28], F32)
    k.dma("sp", bfm[:], bfm_d[:, :], w=[bfmB])
    m = ml_inproj(k, h, w_main, w_gate, bfm, bfmB, brow, need_mo=False)
    ci, ciB = k.sb("cinit", [128, 4, 129], F32)
    k.dma("sp", ci[:], st_in[:, :, :], w=[ciB])
    hm_rot = Rot(k, "hm_", [128, 512], F32, 3)
    state = {}

    def emit_out(c, hd, pO, pOB, t1, t1B):
        if hd == 0:
            t, tB = hm_rot.next()
            state["t"] = (t, tB)
            k.dma("sp", t[:], hd1[c * 128:(c + 1) * 128, :], w=[tB])
        t, tB = state["t"]
        hs = slice(hd * 128, (hd + 1) * 128)
        k.stt(t[:, hs], pO[:, 0:128], t1[:, 0:1], t[:, hs], ALU.mult, ALU.add, list(pOB) + [t1B, tB], [tB])
        if hd == 3:
            k.dma("sp", hm_o[c * 128:(c + 1) * 128, :], t[:], r=[tB])

    ml_scan(k, m, 1, (ci, ciB), emit_out, None)
    if solo:
        k.finish()
        return nc, k


def build_b1b(k=None, io=None):
    solo = k is None
    if solo:
        nc = bass.Bass("TRN2", target_bir_lowering=False)
        k = K(nc)
    io = IO(k, io)
    h = io.i("h", [NT, D])
    hm = io.i("hm", [NT, 512])
    mo = io.i("mo", [NT, 512])
    y_na = io.i("y_na", [NT, 512])
    mlw = io.i("mlw", [1, 512])
    w_out = io.i("w_out", [D, D])
    vecs = io.i("vecs", [3, D])
    h1_o = io.o("h1", [NT, D])
    if solo:
        k.load_consts()
    k.hb_rot = Rot(k, "hb", [128, 1024], BF16, 2)
    k.tail_setup(nr=4, nh=4)
    vb = [k.bc_load("vec%d" % i, vecs[i:i + 1, :], D) for i in range(3)]
    mlwbc, mlwB = k.bc_load("mlwbc", mlw[0:1, :], 512)
    wo, woB = k.load_w("wo", w_out, 0, 1024)
    hm_rot = Rot(k, "hm_", [128, 512], F32, 4)
    mo_rot = Rot(k, "mot_", [128, 512], F32, 4)
    yn_rot = Rot(k, "ynl_", [128, 512], F32, 4)
    yt_rot = Rot(k, "ytk_", [128, 1024], BF16, 4)
    st4_rot = Rot(k, "st4_", [128, 4, 6], F32, 4)
    mv4_rot = Rot(k, "mv4_", [128, 4, 2], F32, 4)
    rs4_rot = Rot(k, "rs4_", [128, 4], F32, 4)
    yT_rot = Rot(k, "yT", [128, 8, 128], BF16, 4)
    msg_rot = Rot(k, "msg", [128, 512], F32, 4)
    ctx = {}

    def st_a1(ti):
        rows = slice(ti * 128, (ti + 1) * 128)
        t, tB = hm_rot.next()
        k.dma("sp", t[:], hm[rows, :], w=[tB])
        mt, mtB = mo_rot.next()
        k.dma("sp", mt[:], mo[rows, :], w=[mtB])
        yn, ynB = yn_rot.next()
        k.dma("sp", yn[:], y_na[rows, :], w=[ynB])
        yt, ytB = yt_rot.next()
        k.cp("pool", yt[:, 512:1024], yn[:], [ynB], [ytB])
        st4, st4B = st4_rot.next()
        mv4, mv4B = mv4_rot.next()
        rs4, rs4B = rs4_rot.next()
        for j in range(4):
            k.P.op("dve", lambda e, j=j, st4=st4, t=t: e.bn_stats(st4[:, j, :], t[:, j * 128:(j + 1) * 128]), [tB], [st4B])
        for j in range(4):
            k.P.op("dve", lambda e, j=j, st4=st4, mv4=mv4: e.bn_aggr(mv4[:, j, :], st4[:, j, :]), [st4B], [mv4B])
        k.act(rs4[:], mv4[:, :, 1], AF.Sqrt, [mv4B], [rs4B], bias=EPS, scale=1.0)
        k.act(mt[:], mt[:], AF.Sigmoid, [mtB], [mtB])
        msg, msgB = msg_rot.next()
        k.tt("pool", msg[:], mt[:], mlwbc[:], ALU.mult, [mtB, mlwB], [msgB])
        ctx[ti] = (t, tB, yt, ytB, mv4, mv4B, rs4, rs4B, msg, msgB)

    def st_a2(ti):
        t, tB, yt, ytB, mv4, mv4B, rs4, rs4B, msg, msgB = ctx[ti]
        k.recip(rs4[:], rs4[:], [rs4B], [rs4B])
        for j in range(4):
            js = slice(j * 128, (j + 1) * 128)
            k.ts("dve", t[:, js], t[:, js], mv4[:, j, 0:1], rs4[:, j:j + 1], ALU.subtract, ALU.mult, [tB, mv4B, rs4B], [tB])
        k.tt("dve", yt[:, 0:512], t[:], msg[:], ALU.mult, [tB, msgB], [ytB])

    def st_b(ti):
        t, tB, yt, ytB = ctx.pop(ti)[0:4]
        yT, yTB = yT_rot.next()
        pb, pbB = k.pb()
        for kc in range(8):
            k.tr(pb[:, kc, :], yt[:, kc * 128:(kc + 1) * 128], [ytB], [pbB])
        k.cp("act", yT[:], pb[:, :, :], [pbB], [yTB])
        pss = []
        for cg in range(2):
            p, pB = k.ps()
            for kc in range(8):
                k.mm(p[:, :], yT[:, kc, :], wo[:, kc, cg * 512:(cg + 1) * 512], kc == 0, kc == 7, [yTB, woB[kc]], [pB])
            pss.append((p, pB))
        if io.m is not None and "h1T" in io.m:
            k.res_ln_tail(pss, vb[0][0], vb[0][1], h, ti, vb[1][0], vb[1][1], vb[2][0], vb[2][1], h1_o, io.m["h1T"][0], io.m["h1T"][1][ti])
        else:
            k.res_ln_tail(pss, vb[0][0], vb[0][1], h, ti, vb[1][0], vb[1][1], vb[2][0], vb[2][1], h1_o)

    for step in range(NTL + 2):
        if step < NTL:
            st_a1(step)
        if 1 <= step <= NTL:
            st_a2(step - 1)
        if step >= 2:
            st_b(step - 2)
    k.tail_flush()
    if solo:
        k.finish()
        return nc, k


def build_b2(k=None, io=None):
    solo = k is None
    if solo:
        nc = bass.Bass("TRN2", target_bir_lowering=False)
        k = K(nc)
    io = IO(k, io)
    h1_o = io.i("h1", [NT, D])
    vecs = io.i("vecs", [3, D])
    mem = io.i("mem", [256, D])
    w_xq = io.i("w_xq", [D, D])
    w_xkv = io.i("w_xkv", [D, 2 * D])
    w_xo = io.i("w_xo", [D, D])
    h2_o = io.o("h2", [NT, D])
    if solo:
        k.load_consts()
    k.hb_rot = Rot(k, "hb", [128, 1024], BF16, 2)
    k.tail_setup()
    if io.m is not None and "h1T" in io.m:
        hT, hTB = io.m["h1T"]
    else:
        hT, _ = k.sb("hT", [128, 8, NT], BF16)
        hTB = [Buf("hT%d" % i) for i in range(NTL)]
        k.load_hT(h1_o, NTL, hT, hTB)
    vb = [None, None, None] + [k.bc_load("vec%d" % i, vecs[i:i + 1, :], D) for i in range(3)]
    yT_rot = Rot(k, "yT", [128, 8, 128], BF16, 2)
    wq, wqB = k.load_w("wxq", w_xq, 0, 1024)
    wkv, wkvB = k.load_w("wxkv", w_xkv, 0, 2048)
    wxo, wxoB = k.load_w("wxo", w_xo, 0, 1024)
    memT, memTB = k.sb("memT", [128, 8, 256], BF16)
    memTBs = [memTB, Buf("memT1")]
    k.load_hT(mem, 2, memT, memTBs)
    kTm, kTmB = k.sb("kTm", [128, 8, 256], BF16)
    vma, vmaB = k.sb("vma", [128, 2, 4, 258], BF16)
    k.memset("pool", vma[:, :, :, 256:258], 1.0, [vmaB])
    for ch in range(8):
        p, pB = k.ps()
        for kc in range(8):
            k.mm(p[:, 0:256], wkv[:, kc, ch * 128:(ch + 1) * 128], memT[:, kc, :], kc == 0, kc == 7, [wkvB[kc]] + memTBs, [pB])
        k.cp("act", kTm[:, ch, :], p[:, 0:256], [pB], [kTmB])
    for mt in range(2):
        for cg in range(2):
            p, pB = k.ps()
            for kc in range(8):
                k.mm(p[:, :], memT[:, kc, mt * 128:(mt + 1) * 128], wkv[:, kc, 1024 + cg * 512:1024 + (cg + 1) * 512], kc == 0, kc == 7,
                     [wkvB[kc]] + memTBs, [pB])
            k.cp("dve", vma[:, mt, 2 * cg:2 * cg + 2, 0:256], p[:, :].rearrange("p (a b) -> p a b", b=256), [pB], [vmaB])
    qx_rot = Rot(k, "qxT", [128, 8, 512], BF16, 2)
    px_rot = Rot(k, "pxT", [128, 2, 512], BF16, 3)
    ox_rot = Rot(k, "oxt", [128, 4, 1024], BF16, 2)
    rd_rot = Rot(k, "rdx", [128, 1], F32, 4)

    def qx_stage(blk):
        cs = slice(blk * 512, (blk + 1) * 512)
        hr = hTB[blk * 4:(blk + 1) * 4]
        qx, qxB = qx_rot.next()
        for ch in range(8):
            p, pB = k.ps()
            for kc in range(8):
                k.mm(p[:, :], wq[:, kc, ch * 128:(ch + 1) * 128], hT[:, kc, cs], kc == 0, kc == 7, [wqB[kc]] + hr, [pB])
            k.act(qx[:, ch, :], p[:, :], AF.Identity, [pB], [qxB], scale=1.0 / 16.0)
        return qx, qxB

    def heads_stage(blk, qx, qxB):
        ox, oxB = ox_rot.next()

        def xa_s(hd):
            px, pxB = px_rot.next()
            for mt in range(2):
                p, pB = k.ps()
                for half in range(2):
                    k.mm(p[:, :], kTm[:, 2 * hd + half, mt * 128:(mt + 1) * 128], qx[:, 2 * hd + half, :], half == 0, half == 1,
                         [kTmB, qxB], [pB])
                k.act(px[:, mt, :], p[:, :], AF.Exp, [pB], [pxB])
            return px, pxB

        def xa_v(hd, px, pxB):
            for st in range(4):
                pO, pOB = k.ps()
                for mt in range(2):
                    k.mm(pO[:, 0:257], px[:, mt, st * 128:(st + 1) * 128], vma[:, mt, hd, 0:257], mt == 0, mt == 1, [pxB, vmaB], [pOB])
                rd, rdB = rd_rot.next()
                k.recip(rd[:], pO[:, 256:257], [pOB], [rdB])
                k.ts("dve", ox[:, st, hd * 256:(hd + 1) * 256], pO[:, 0:256], rd[:, 0:1], None, ALU.mult, None, [pOB, rdB], [oxB])

        nxt = xa_s(0)
        for hd in range(4):
            curp = nxt
            if hd < 3:
                nxt = xa_s(hd + 1)
            xa_v(hd, curp[0], curp[1])
        return ox, oxB

    def out_stage(blk, ox, oxB):
        for st in range(4):
            ti = blk * 4 + st
            yT, yTB = yT_rot.next()
            pb, pbB = k.pb()
            for kc in range(8):
                k.tr(pb[:, kc, :], ox[:, st, kc * 128:(kc + 1) * 128], [oxB], [pbB])
            k.cp("act", yT[:], pb[:, :, :], [pbB], [yTB])
            pss = []
            for cg in range(2):
                p, pB = k.ps()
                for kc in range(8):
                    k.mm(p[:, :], yT[:, kc, :], wxo[:, kc, cg * 512:(cg + 1) * 512], kc == 0, kc == 7, [yTB, wxoB[kc]], [pB])
                pss.append((p, pB))
            k.res_ln_tail(pss, vb[3][0], vb[3][1], h1_o, ti, vb[4][0], vb[4][1], vb[5][0], vb[5][1], h2_o)

    q = qx_stage(0)
    for blk in range(4):
        o = heads_stage(blk, q[0], q[1])
        if blk < 3:
            q = qx_stage(blk + 1)
        out_stage(blk, o[0], o[1])
    k.tail_flush()
    if solo:
        k.finish()
        return nc, k


def build_c(k=None, io=None):
    solo = k is None
    if solo:
        nc = bass.Bass("TRN2", target_bir_lowering=False)
        k = K(nc)
    io = IO(k, io)
    h2 = io.i("h2", [NT, D])
    h2h = io.i("h2h", [128, D])
    w_upr = io.i("w_upr", [44, 128, 8, 128])
    bupf_d = io.i("bupf", [128, 44])
    bupf2_d = io.i("bupf2", [128, 88])
    wdwf_d = io.i("wdwf", [128, 44, 3])
    bdwf_d = io.i("bdwf", [128, 44])
    w_down = io.i("w_down", [DFF, D])
    vecs = io.i("vecs", [3, D])
    h3_o = io.o("h3", [NT, D])
    if solo:
        k.load_consts()
    k.hb_rot = Rot(k, "hb", [128, 1024], BF16, 3)
    k.tail_setup(nr=6)
    hT, _ = k.sb("hT", [128, 8, NT + 128], BF16)
    hTB = [Buf("hT%d" % i) for i in range(17)]
    k.load_hT(h2, NTL, hT, hTB)
    k.load_hT(h2h, 1, hT, hTB, tile0=16)
    vb = [k.bc_load("vec%d" % i, vecs[i:i + 1, :], D) for i in range(3)]
    small = {}
    for nm, src, shp in (("bupf", bupf_d, [128, 44]), ("bupf2", bupf2_d, [128, 88]), ("wdwf", wdwf_d, [128, 44, 3]), ("bdwf", bdwf_d, [128, 44])):
        t, b = k.sb(nm, shp, F32)
        k.dma("sp", t[:], src, w=[b])
        small[nm] = (t, b)
    bupf, bupfB = small["bupf"]
    bupf2, bupf2B = small["bupf2"]
    wdwf, wdwfB = small["wdwf"]
    bdwf, bdwfB = small["bdwf"]
    yT, _ = k.sb("yTf", [128, 22, 1024], BF16)
    yTBs = [[Buf("yTf%d_%d" % (i, b)) for b in range(2)] for i in range(22)]
    wu_rot = Rot(k, "wu", [128, 8, 128], BF16, 4)
    x_rot = Rot(k, "xt", [128, 1026], F32, 3)
    a_rot = Rot(k, "at", [128, 1024], F32, 4)
    g_rot = Rot(k, "gt", [128, 1024], F32, 2)
    wd_rot = Rot(k, "wdp", [128, 512], BF16, 6)
    for sb_ in range(2):
        T0 = sb_ * 1024
        pH, pHB = k.psx()
        lo = T0 - 1 if sb_ > 0 else T0
        hi = T0 + 1024
        hr = [hTB[sb_ * 8 + j] for j in range(8)]
        hread = hr + [hTB[(T0 + 1024) // 128]] + ([hTB[(T0 - 1) // 128]] if sb_ > 0 else [])
        for cp_ in range(22):
            av = []
            for part in range(2):
                ch = cp_ + 22 * part
                wu, wuB = wu_rot.next()
                k.dma("pool", wu[:], w_upr[ch, :, :, :], w=[wuB])
                pp = [k.ps(), k.ps()]
                for bi in range(2):
                    cs = slice(T0 + bi * 512, T0 + (bi + 1) * 512)
                    for kc in range(8):
                        k.mm(pp[bi][0][:, :], wu[:, kc, :], hT[:, kc, cs], kc == 0, kc == 7, [wuB] + hr[bi * 4:(bi + 1) * 4], [pp[bi][1]])
                for kc in range(8):
                    k.mm(pH[:, 2 * ch:2 * ch + 1], wu[:, kc, :], hT[:, kc, lo:lo + 1], kc == 0, kc == 7, [wuB] + hread, [pHB])
                for kc in range(8):
                    k.mm(pH[:, 2 * ch + 1:2 * ch + 2], wu[:, kc, :], hT[:, kc, hi:hi + 1], kc == 0, kc == 7, [wuB] + hread, [pHB])
                x, xB = x_rot.next()
                for bi in range(2):
                    k.act(x[:, 1 + bi * 512:513 + bi * 512], pp[bi][0][:, :], AF.Identity, [pp[bi][1], bupfB], [xB],
                          bias=bupf[:, ch:ch + 1], scale=1.0)
                a, aB = a_rot.next()
                k.act(a[:], x[:, 1:1025], AF.Identity, [xB, wdwfB, bdwfB], [aB], bias=bdwf[:, ch:ch + 1], scale=wdwf[:, ch, 1:2])
                k.tt("dve", x[:, 0:1026:1025], pH[:, 2 * ch:2 * ch + 2], bupf2[:, 2 * ch:2 * ch + 2], ALU.add, [pHB, bupf2B], [xB])
                if sb_ == 0:
                    k.memset("dve", x[:, 0:1], 0.0, [xB])
                k.stt(a[:], x[:, 0:1024], wdwf[:, ch, 0:1], a[:], ALU.mult, ALU.add, [xB, wdwfB, aB], [aB])
                k.stt(a[:], x[:, 2:1026], wdwf[:, ch, 2:3], a[:], ALU.mult, ALU.add, [xB, wdwfB, aB], [aB])
                av.append((a, aB))
            g, gB = g_rot.next()
            k.act(g[:], av[0][0][:], AF.Gelu, [av[0][1]], [gB])
            k.tt("dve", yT[:, cp_, :], g[:], av[1][0][:], ALU.mult, [gB, av[1][1]], yTBs[cp_])
        for grp in range(2):
            rts = [k.r_rot.next() for _ in range(4)]
            for cg in range(2):
                pl = [k.ps() for _ in range(4)]
                for cc in range(22):
                    wd, wdB = wd_rot.next()
                    k.dma("pool", wd[:], w_down[cc * 128:(cc + 1) * 128, cg * 512:(cg + 1) * 512], w=[wdB])
                    for s4 in range(4):
                        k.mm(pl[s4][0][:, :], yT[:, cc, grp * 512 + s4 * 128:grp * 512 + (s4 + 1) * 128], wd[:, :], cc == 0, cc == 21,
                             [yTBs[cc][grp], wdB], [pl[s4][1]])
                for s4 in range(4):
                    k.tail_fill(rts[s4][0], rts[s4][1], cg, pl[s4][0], pl[s4][1], vb[0][0], vb[0][1])
            for s4 in range(4):
                ti = sb_ * 8 + grp * 4 + s4
                k.tail_finish(rts[s4][0], rts[s4][1], h2, ti, vb[1][0], vb[1][1], vb[2][0], vb[2][1], h3_o)
    k.tail_flush()
    if solo:
        k.finish()
        return nc, k


PAIRS = [[0, 1], [2, 3], [4, 5], [6, 7]]


def xchg(k, src, dst, srcB, dstB):
    k.P.cc(lambda e: e.collective_compute("AllReduce", ALU.add, replica_groups=PAIRS, ins=[src.opt()], outs=[dst.opt()]),
           [srcB], [dstB], inc=1)


def phase_halo(k, h, hh, cc_src, cc_dst):
    k.push()
    srcB, dstB = Buf("hsrc"), Buf("hdst")
    rot = Rot(k, "hx", [128, 1024], F32, 4)
    mine = []
    for j, r0 in enumerate((1920, 1792)):
        t, tB = rot.next()
        k.dma("sp", t[:], h[r0:r0 + 128, :], w=[tB])
        k.dma("sp", cc_src[j * 128:(j + 1) * 128, :], t[:], r=[tB], w=[srcB])
        mine.append((t, tB))
    xchg(k, cc_src, cc_dst, srcB, dstB)
    for j in range(2):
        u, uB = rot.next()
        k.dma("sp", u[:], cc_dst[j * 128:(j + 1) * 128, :], r=[dstB], w=[uB])
        k.tt("dve", u[:], u[:], mine[j][0][:], ALU.subtract, [uB, mine[j][1]], [uB])
        k.dma("sp", hh[j * 128:(j + 1) * 128, :], u[:], r=[uB])
    k.pop()


def phase_row(k, h2, h2h, cc_src, cc_dst):
    k.push()
    srcB, dstB = Buf("rsrc"), Buf("rdst")
    t, tB = k.sb("rowt", [1, 1024], F32)
    u, uB = k.sb("rowu", [1, 1024], F32)
    k.dma("sp", t[:], h2[2047:2048, :], w=[tB])
    k.dma("sp", cc_src[0:1, :], t[:], r=[tB], w=[srcB])
    xchg(k, cc_src, cc_dst, srcB, dstB)
    k.dma("sp", u[:], cc_dst[0:1, :], r=[dstB], w=[uB])
    k.tt("dve", u[:], u[:], t[:], ALU.subtract, [uB, tB], [uB])
    k.dma("sp", h2h[0:1, :], u[:], r=[uB])
    k.pop()


def phase_B(k, io):
    h, w_main, w_gate, bfm_d, brow = io["h"], io["w_main"], io["w_gate"], io["bfm"], io["brow"]
    hd1, mo_o, hm_o, st_src, st_dst = io["hd1"], io["mo"], io["hm"], io["st_src"], io["st_dst"]
    bfm, bfmB = k.sb("bfm", [128, 28], F32)
    k.dma("sp", bfm[:], bfm_d[:, :], w=[bfmB])
    m = ml_inproj(k, h, w_main, w_gate, bfm, bfmB, brow, need_mo=True, ext_hT=io.get("hT"))
    for ti in range(NTL):
        k.dma("pool", mo_o[ti * 128:(ti + 1) * 128, :], m["mo"][:, ti, :], r=[m["tokBs"][ti]])
    hd_rot = Rot(k, "hdt", [128, 512], F32, 3)
    hd1B = [Buf("hd1_%d" % c) for c in range(NTL)]
    srcB, dstB = Buf("ssrc"), Buf("sdst")
    state = {}

    def emit_out1(c, hd, pO, pOB, t1, t1B):
        if hd == 0:
            state["t"] = hd_rot.next()
        t, tB = state["t"]
        k.ts("dve", t[:, hd * 128:(hd + 1) * 128], pO[:, 0:128], t1[:, 0:1], None, ALU.mult, None, list(pOB) + [t1B], [tB])
        if hd == 3:
            k.dma("sp", hd1[c * 128:(c + 1) * 128, :], t[:], r=[tB], w=[hd1B[c]])

    sto, stoB = k.sb("sto", [128, 4, 129], F32)

    def emit_state(hd, Dt, DB, lam_ap, lamB):
        k.act(sto[:, hd, :], Dt[:], AF.Identity, [DB, lamB], [stoB], scale=lam_ap)
        if hd == 3:
            k.dma("sp", st_src[:, :], sto[:].rearrange("p a b -> p (a b)"), r=[stoB], w=[srcB])

    ml_scan(k, m, 0, None, emit_out1, emit_state)
    xchg(k, st_src, st_dst, srcB, dstB)
    ci, ciB = k.sb("cinit", [128, 4, 129], F32)
    k.dma("sp", ci[:].rearrange("p a b -> p (a b)"), st_dst[:, :], r=[dstB], w=[ciB])
    k.tt("dve", ci[:], ci[:], sto[:], ALU.subtract, [ciB, stoB], [ciB])

    def emit_out2(c, hd, pO, pOB, t1, t1B):
        if hd == 0:
            t, tB = hd_rot.next()
            state["t"] = (t, tB)
            k.dma("sp", t[:], hd1[c * 128:(c + 1) * 128, :], r=[hd1B[c]], w=[tB])
        t, tB = state["t"]
        hs = slice(hd * 128, (hd + 1) * 128)
        k.stt(t[:, hs], pO[:, 0:128], t1[:, 0:1], t[:, hs], ALU.mult, ALU.add, list(pOB) + [t1B, tB], [tB])
        if hd == 3:
            k.dma("sp", hm_o[c * 128:(c + 1) * 128, :], t[:], r=[tB])

    ml_scan(k, m, 1, (ci, ciB), emit_out2, None)


def build_fused(depth=DEPTH, stop=None):
    nc = bass.Bass("TRN2", target_bir_lowering=False)
    k = K(nc)
    k.load_consts()
    x = k.inp("x", [NT, D])
    lnw = k.inp("lnw", [1, D])
    lnb = k.inp("lnb", [1, D])
    mem = k.inp("mem", [256, D])
    shapes = dict(w_main=[D, 3584], w_gate=[D, 16], bfm=[128, 28], brow=[1, 3600], rpbt=[8, 128, 23, 128], mlw=[1, 512],
                  w_out=[D, D], vecs1=[3, D], w_xq=[D, D], w_xkv=[D, 2 * D], w_xo=[D, D], vecs2=[3, D],
                  w_upr=[44, 128, 8, 128], bupf=[128, 44], bupf2=[128, 88], wdwf=[128, 44, 3], bdwf=[128, 44],
                  w_down=[DFF, D], vecs3=[3, D])
    W = [{n: k.inp("%s_%d" % (n, l), shp) for n, shp in shapes.items()} for l in range(depth)]
    out = k.out("out", [NT, D])
    S = k.scratch
    hA, hh = S("s_hA", [NT, D]), S("s_hh", [256, D])
    y_na, hd1, mo, hm = S("s_yna", [NT, 512]), S("s_hd1", [NT, 512]), S("s_mo", [NT, 512]), S("s_hm", [NT, 512])
    h1, h2, h2h = S("s_h1", [NT, D]), S("s_h2", [NT, D]), S("s_h2h", [128, D])
    cc_h_src, cc_h_dst = S("cc_h_src", [256, D]), S("cc_h_dst", [256, D])
    cc_r_src, cc_r_dst = S("cc_r_src", [1, D]), S("cc_r_dst", [1, D])
    cc_s_src, cc_s_dst = S("cc_s_src", [128, 516]), S("cc_s_dst", [128, 516])

    class _Stop(Exception):
        pass

    def chk(name):
        if stop == name:
            raise _Stop()

    try:
        k.push()
        build_pre(k, dict(x=x, lnw=lnw, lnb=lnb, h=hA))
        k.pop()
        phase_halo(k, hA, hh, cc_h_src, cc_h_dst)
        chk("pre")
        for l in range(depth):
            w = W[l]
            k.push()
            hTs = (k.alloc([128, 8, NT + 256], BF16), [Buf("hTs%d_%d" % (l, i)) for i in range(18)])
            k.push()
            build_a1(k, dict(h=hA, hh=hh, w_main=w["w_main"], bfm=w["bfm"], brow=w["brow"], rpbt=w["rpbt"], y_na=y_na, hT=hTs))
            k.pop()
            chk("a1")
            k.push()
            phase_B(k, dict(h=hA, w_main=w["w_main"], w_gate=w["w_gate"], bfm=w["bfm"], brow=w["brow"], hd1=hd1, mo=mo, hm=hm,
                            st_src=cc_s_src, st_dst=cc_s_dst, hT=hTs))
            k.pop()
            k.pop()
            chk("B")
            k.push()
            h1Ts = (k.alloc([128, 8, NT], BF16), [Buf("h1Ts%d_%d" % (l, i)) for i in range(NTL)])
            k.push()
            build_b1b(k, dict(h=hA, hm=hm, mo=mo, y_na=y_na, mlw=w["mlw"], w_out=w["w_out"], vecs=w["vecs1"], h1=h1, h1T=h1Ts))
            k.pop()
            chk("b1b")
            k.push()
            build_b2(k, dict(h1=h1, vecs=w["vecs2"], mem=mem, w_xq=w["w_xq"], w_xkv=w["w_xkv"], w_xo=w["w_xo"], h2=h2, h1T=h1Ts))
            k.pop()
            k.pop()
            phase_row(k, h2, h2h, cc_r_src, cc_r_dst)
            chk("b2")
            k.push()
            build_c(k, dict(h2=h2, h2h=h2h, w_upr=w["w_upr"], bupf=w["bupf"], bupf2=w["bupf2"], wdwf=w["wdwf"], bdwf=w["bdwf"],
                            w_down=w["w_down"], vecs=w["vecs3"], h3=(out if l == depth - 1 else hA)))
            k.pop()
            chk("c")
            if l < depth - 1:
                phase_halo(k, hA, hh, cc_h_src, cc_h_dst)
    except _Stop:
        pass
    k.finish()
    return nc, k


_PROGS = {}
_BUILDERS = {}
_DBG = None


def _run(name, in_maps):
    if name not in _PROGS:
        _PROGS[name] = _BUILDERS[name]()
    nc, k = _PROGS[name]
    res = run_bass_kernel_spmd(nc, in_maps, core_ids=list(range(8)))
    if _DBG is not None:
        _DBG(name, res.results)
    return res.results


def _na_tables(odd):
    def gpos(j):
        j = np.asarray(j)
        m = j - 2048
        pt = np.where(m < 128, 1920 + m, 1792 + (m - 128))
        if not odd:
            return np.where(j < 2048, j, 4095 - pt)
        return np.where(j < 2048, 4095 - j, pt)

    tiles = [(6, 6 + d) for d in (-2, -1, 0, 1, 2)]
    tiles += [(0, kt) for kt in (0, 1, 2, 3)] + [(1, kt) for kt in (0, 1, 2, 3)]
    tiles += [(14, kt) for kt in (12, 13, 14, 15, 16)] + [(15, kt) for kt in (13, 14, 15, 16, 17)]
    dr = np.zeros((23, 128, 128), np.int64)
    dc = np.zeros((23, 128, 128), np.int64)
    ok = np.zeros((23, 128, 128), bool)
    ar = np.arange(128)
    for i, (qp, kt) in enumerate(tiles):
        gk = gpos(kt * 128 + ar)[:, None]
        gq = gpos(qp * 128 + ar)[None, :]
        krow, kcol, qrow, qcol = gk // 64, gk % 64, gq // 64, gq % 64
        rs = np.clip(qrow - 4, 0, 56)
        cs = np.clip(qcol - 8, 0, 48)
        ok[i] = (krow >= rs) & (krow < rs + 8) & (kcol >= cs) & (kcol < cs + 16)
        dr[i] = np.clip(krow - qrow + 7, 0, 14)
        dc[i] = np.clip(kcol - qcol + 15, 0, 30)
    return dr, dc, ok


def kernel(x, mem, ln_in_w, ln_in_b, w_in, b_in, ml_norm_w, na_rpb, w_mix_out, b_mix_out,
           ln1_w, ln1_b, w_xq, w_xkv, w_xo, b_xo, ln2_w, ln2_b,
           w_up, b_up, w_dw, b_dw, w_down, b_down, ln3_w, ln3_b):
    f = lambda a: np.ascontiguousarray(np.asarray(a, dtype=np.float32))
    x, mem = f(x), f(mem)
    cores = range(8)
    consts = np.zeros((128, 4, 128), np.float32)
    ar = np.arange(128)
    consts[:, 0, :] = np.eye(128)
    consts[:, 1, :] = (ar[:, None] <= ar[None, :])
    consts[:, 2, :] = (ar[:, None] >= ar[None, :])
    consts[:, 3, :] = 1.0
    row = lambda v: f(v).reshape(1, -1)

    def loc(c, a):
        return f(a[0:2048]) if c % 2 == 0 else f(a[4095:2047:-1])

    shared = dict(consts=consts, lnw=row(ln_in_w), lnb=row(ln_in_b))
    par = [dict(), dict()]
    tabs = [_na_tables(0), _na_tables(1)]
    for l in range(DEPTH):
        wi, bi = f(w_in[l]), f(b_in[l])
        b_main = np.concatenate([bi[0:2048], bi[2064:3600]])
        gsel = [np.arange(2048, 2064), np.concatenate([np.arange(2056, 2064), np.arange(2048, 2056)])]
        sfx = "_%d" % l
        shared["w_main" + sfx] = f(np.concatenate([wi[:, 0:2048], wi[:, 2064:3600]], axis=1))
        shared["bfm" + sfx] = f(b_main.reshape(28, 128).T)
        shared["mlw" + sfx] = row(ml_norm_w[l])
        shared["w_out" + sfx] = f(w_mix_out[l])
        shared["vecs1" + sfx] = f(np.stack([b_mix_out[l], ln1_w[l], ln1_b[l]]))
        shared["w_xq" + sfx] = f(w_xq[l])
        shared["w_xkv" + sfx] = f(w_xkv[l])
        shared["w_xo" + sfx] = f(w_xo[l])
        shared["vecs2" + sfx] = f(np.stack([b_xo[l], ln2_w[l], ln2_b[l]]))
        shared["w_upr" + sfx] = f(f(w_up[l]).reshape(8, 128, 44, 128).transpose(2, 1, 0, 3))
        bupf = f(f(b_up[l]).reshape(44, 128).T)
        shared["bupf" + sfx] = bupf
        shared["bupf2" + sfx] = f(np.repeat(bupf, 2, axis=1))
        shared["bdwf" + sfx] = f(f(b_dw[l]).reshape(44, 128).T)
        shared["w_down" + sfx] = f(w_down[l])
        shared["vecs3" + sfx] = f(np.stack([b_down[l], ln3_w[l], ln3_b[l]]))
        wd3 = f(w_dw[l]).reshape(3, 44, 128).transpose(2, 1, 0)
        rp = f(na_rpb[l])
        for p in range(2):
            par[p]["w_gate" + sfx] = f(wi[:, gsel[p]])
            par[p]["brow" + sfx] = f(np.concatenate([b_main, bi[gsel[p]]]).reshape(1, -1))
            par[p]["wdwf" + sfx] = f(wd3 if p == 0 else wd3[:, :, ::-1])
            dr, dc, ok = tabs[p]
            t = np.where(ok[None], rp[:, dr, dc], np.float32(NEG)).astype(np.float32)
            par[p]["rpbt" + sfx] = f(t.transpose(0, 2, 1, 3))
    in_maps = []
    for c in cores:
        m = dict(shared)
        m.update(par[c % 2])
        m["x"] = loc(c, x[c // 2])
        m["mem"] = mem[c // 2]
        in_maps.append(m)
    if _DBG is not None and _DBG("in_maps", in_maps):
        return None
    res = _run("fused", in_maps)
    out = np.zeros((4, 4096, D), np.float32)
    for c in cores:
        if c % 2 == 0:
            out[c // 2, 0:2048] = res[c]["out"]
        else:
            out[c // 2, 2048:4096] = res[c]["out"][::-1]
    return out


_BUILDERS.update(fused=build_fused, pre=build_pre, a1=build_a1, a2=build_a2, b1=build_b1, b1b=build_b1b, b2=build_b2, c=build_c)
```

```python
import numpy as np
import concourse.bass as bass
import concourse.mybir as mybir
from concourse.bass_utils import run_bass_kernel_spmd

F32 = mybir.dt.float32
BF16 = mybir.dt.bfloat16
AF = mybir.ActivationFunctionType
ALU = mybir.AluOpType

ENGS = ("pe", "act", "dve", "pool", "sp")
DMA_K = 12

D = 1024
NT = 2048
NTL = 16
DEPTH = 4
DFF = 2816
ALPHA = (2.0 * DEPTH) ** 0.25
EPS = 1e-5
QS = 128.0 ** -0.5
NEG = -30000.0
SB_BASE = 16640
SB_END = 212992


class Buf:
    __slots__ = ("name", "w", "r", "excl")

    def __init__(self, name="", excl=False):
        self.name = name
        self.w = None
        self.r = []
        self.excl = excl


class Prog:
    def __init__(self, nc):
        self.nc = nc
        self.ops = {e: [] for e in ENGS}
        self.cnt = {e: 0 for e in ENGS}
        self.waited = {e: {} for e in ENGS}
        self.dma_cnt = {e: 0 for e in ENGS}
        self.fence = {}
        self.ncc = 0
        self.cc_vals = {}

    def barrier(self):
        f = {}
        for e in ENGS:
            if self.cnt[e] > 0:
                f[e] = self.cnt[e]
        for q in ENGS:
            n = self.dma_cnt[q]
            for j in range(min(n, DMA_K)):
                last_i = j + ((n - 1 - j) // DMA_K) * DMA_K
                f[("d", q, j)] = 16 * (last_i // DMA_K + 1)
        for key, v in self.cc_vals.items():
            f[key] = v
        self.fence = f

    def _deps(self, eng, reads, writes):
        need = dict(self.fence)

        def add(tok):
            k, v = tok
            if need.get(k, 0) < v:
                need[k] = v

        for b in reads:
            if b.excl:
                if b.w is not None and b.w[0] != eng:
                    add(b.w)
                continue
            if b.w is not None and not (eng == "pe" and b.w[0] == "pe"):
                add(b.w)
        for b in writes:
            if b.excl:
                if b.w is not None and b.w[0] != eng:
                    add(b.w)
                continue
            if b.w is not None and not (eng == "pe" and b.w[0] == "pe"):
                add(b.w)
            for t in b.r:
                if t[0] != eng:
                    add(t)
        waits = []
        wd = self.waited[eng]
        for k, v in need.items():
            if wd.get(k, 0) < v:
                wd[k] = v
                waits.append((k, v))
        return waits

    def _commit(self, tok, reads, writes):
        for b in reads:
            if b.excl:
                b.w = tok
            else:
                b.r.append(tok)
        for b in writes:
            b.w = tok
            b.r = []

    def op(self, eng, fn, reads=(), writes=()):
        waits = self._deps(eng, reads, writes)
        self.cnt[eng] += 1
        tok = (eng, self.cnt[eng])
        self.ops[eng].append((waits, fn, (eng, 1)))
        self._commit(tok, reads, writes)

    def dma(self, q, out, in_, reads=(), writes=()):
        waits = self._deps(q, reads, writes)
        i = self.dma_cnt[q]
        self.dma_cnt[q] += 1
        j = i % DMA_K
        val = 16 * (i // DMA_K + 1)
        key = ("d", q, j)
        if val > 16:
            wd = self.waited[q]
            if wd.get(key, 0) < val - 16:
                wd[key] = val - 16
                waits.append((key, val - 16))
        self.ops[q].append((waits, lambda e: e.dma_start(out=out, in_=in_), (key, 16)))
        self._commit((key, val), reads, writes)

    def dma_fn(self, q, fn, reads=(), writes=()):
        waits = self._deps(q, reads, writes)
        i = self.dma_cnt[q]
        self.dma_cnt[q] += 1
        j = i % DMA_K
        val = 16 * (i // DMA_K + 1)
        key = ("d", q, j)
        if val > 16:
            wd = self.waited[q]
            if wd.get(key, 0) < val - 16:
                wd[key] = val - 16
                waits.append((key, val - 16))
        self.ops[q].append((waits, fn, (key, 16)))
        self._commit((key, val), reads, writes)

    def cc(self, fn, reads=(), writes=(), inc=1):
        waits = self._deps("pool", reads, writes)
        self.ncc += 1
        key = ("cc", self.ncc)
        self.cc_vals[key] = inc
        self.ops["pool"].append((waits, fn, (key, inc)))
        self._commit((key, inc), reads, writes)

    def finish(self):
        wd = self.waited["sp"]
        waits = []
        for q in ENGS:
            n = self.dma_cnt[q]
            for j in range(min(n, DMA_K)):
                last_i = j + ((n - 1 - j) // DMA_K) * DMA_K
                val = 16 * (last_i // DMA_K + 1)
                key = ("d", q, j)
                if wd.get(key, 0) < val:
                    wd[key] = val
                    waits.append((key, val))
        for e in ENGS:
            if e != "sp" and self.cnt[e] > 0:
                waits.append((e, self.cnt[e]))
        self.ops["sp"].append((waits, None, None))

    def emit(self):
        import contextlib
        nc = self.nc
        with contextlib.ExitStack() as st:
            sems = {}
            for e in ENGS:
                sems[e] = st.enter_context(nc.semaphore("s_" + e))
            for q in ENGS:
                for j in range(min(self.dma_cnt[q], DMA_K)):
                    sems[("d", q, j)] = st.enter_context(nc.semaphore("d_%s_%d" % (q, j)))
            for j in range(getattr(self, "ncc", 0)):
                sems[("cc", j + 1)] = st.enter_context(nc.semaphore("cc_%d" % j))
            block = st.enter_context(nc.Block())

            def run(ename):
                def body(eng):
                    for waits, fn, inc in self.ops[ename]:
                        for k, v in waits:
                            eng.wait_ge(sems[k], v)
                        if fn is not None:
                            fn(eng).then_inc(sems[inc[0]], inc[1])
                return body

            block.tensor(run("pe"))
            block.scalar(run("act"))
            block.vector(run("dve"))
            block.gpsimd(run("pool"))
            block.sync(run("sp"))


class Rot:
    def __init__(self, k, name, shape, dt, n):
        self.t = [k.alloc(shape, dt) for i in range(n)]
        self.b = [Buf("%s%d" % (name, i)) for i in range(n)]
        self.i = 0

    def next(self):
        k = self.i % len(self.t)
        self.i += 1
        return self.t[k], self.b[k]


class K:
    def __init__(self, nc):
        self.nc = nc
        self.P = Prog(nc)
        self.psf = [nc.alloc_psum_tensor("psf%d" % i, [128, 512], F32) for i in range(6)]
        self.psfB = [Buf("psf%d" % i, excl=True) for i in range(6)]
        self.psb = [nc.alloc_psum_tensor("psb%d" % i, [128, 8, 128], BF16) for i in range(2)]
        self.psbB = [Buf("psb%d" % i, excl=True) for i in range(2)]
        self.pi = 0
        self.pbi = 0
        self.nrot = 5
        self.ln_eng = "pool"
        self.din = {}
        self.dout = {}
        self.sp = SB_BASE
        self.stack = []
        self.nalloc = 0
        self.sp_max = SB_BASE

    def alloc(self, shape, dt):
        n = 1
        for d_ in shape[1:]:
            n *= d_
        nbytes = n * (2 if dt == BF16 else 4)
        nbytes = (nbytes + 63) // 64 * 64
        assert self.sp + nbytes <= SB_END, "SBUF overflow: need %d" % (self.sp + nbytes - SB_END)
        self.nalloc += 1
        t = self.nc.alloc_sbuf_tensor_at("sbt%d" % self.nalloc, list(shape), dt, offset=self.sp)
        self.sp += nbytes
        self.sp_max = max(self.sp_max, self.sp)
        return t

    def push(self):
        self.stack.append(self.sp)

    def pop(self):
        self.P.barrier()
        self.sp = self.stack.pop()

    def scratch(self, name, shape):
        return self.nc.dram_tensor(name, list(shape), F32).ap()

    def inp(self, name, shape):
        t = self.nc.dram_tensor(name, list(shape), F32, kind="ExternalInput").ap()
        self.din[name] = t
        return t

    def out(self, name, shape):
        t = self.nc.dram_tensor(name, list(shape), F32, kind="ExternalOutput").ap()
        self.dout[name] = t
        return t

    def sb(self, name, shape, dt=F32):
        return self.alloc(list(shape), dt), Buf(name)

    def ps(self):
        k = self.pi % self.nrot
        self.pi += 1
        return self.psf[k], self.psfB[k]

    def psx(self):
        return self.psf[5], self.psfB[5]

    def packed(self, tag, banks, nreg):
        key = (tag, tuple(banks), nreg)
        if not hasattr(self, "_packed"):
            self._packed = {}
        if key not in self._packed:
            self._packed[key] = [[(b, [Buf("pk%s_%d_%d" % (tag, b, r)) for r in range(nreg)]) for b in banks], 0]
        ent = self._packed[key]
        b, bufs = ent[0][ent[1] % len(ent[0])]
        ent[1] += 1
        return self.psf[b], bufs, self.psfB[b]

    def pb(self):
        k = self.pbi % 2
        self.pbi += 1
        return self.psb[k], self.psbB[k]

    def mm(self, out, lhsT, rhs, start, stop, r, w):
        self.P.op("pe", lambda e: e.matmul(out, lhsT, rhs, start=start, stop=stop), r, w)

    def tr(self, out, in_, r, w):
        ident = self.ident
        self.P.op("pe", lambda e: e.transpose(out, in_, ident[:]), list(r) + [self.identB], w)

    def act(self, out, in_, func, r, w, bias=None, scale=None):
        kw = {}
        if bias is not None:
            kw["bias"] = bias
        if scale is not None:
            kw["scale"] = scale
        self.P.op("act", lambda e: e.activation(out, in_, func, **kw), r, w)

    def tt(self, eng, out, a, b, op, r, w):
        self.P.op(eng, lambda e: e.tensor_tensor(out, a, b, op), r, w)

    def ts(self, eng, out, in0, s1, s2, op0, op1, r, w):
        if s2 is None:
            self.P.op(eng, lambda e: e.tensor_scalar(out, in0, s1, None, op0), r, w)
        else:
            self.P.op(eng, lambda e: e.tensor_scalar(out, in0, s1, s2, op0, op1), r, w)

    def stt(self, out, in0, scalar, in1, op0, op1, r, w):
        self.P.op("dve", lambda e: e.scalar_tensor_tensor(out, in0, scalar, in1, op0, op1), r, w)

    def cp(self, eng, out, in_, r, w):
        if eng == "act":
            self.P.op("act", lambda e: e.copy(out, in_), r, w)
        else:
            self.P.op(eng, lambda e: e.tensor_copy(out, in_), r, w)

    def recip(self, out, in_, r, w):
        self.P.op("dve", lambda e: e.reciprocal(out, in_), r, w)

    def memset(self, eng, ap, val, w):
        self.P.op(eng, lambda e: e.memset(ap, val), (), w)

    def dma(self, q, out, in_, r=(), w=()):
        self.P.dma(q, out, in_, r, w)

    def load_consts(self):
        c = self.inp("consts", [128, 4, 128])
        self.ident, self.identB = self.sb("ident", [128, 128], BF16)
        self.cf, self.cfB = self.sb("cf", [128, 4, 128], F32)
        self.dma("pool", self.ident[:], c[:, 0, :], w=[self.identB])
        self.dma("sp", self.cf[:], c[:, :, :], w=[self.cfB])

    def bc_load(self, name, row_ap, n):
        t, b = self.sb(name, [128, n], F32)
        self.dma("sp", t[:], row_ap.partition_broadcast(128), w=[b])
        return t, b

    def ln_setup(self, n=2):
        k = self
        self.ln_st = Rot(k, "lnst", [128, 2, 6], F32, n)
        self.ln_mv = Rot(k, "lnmv", [128, 2], F32, n)
        self.ln_rs = Rot(k, "lnrs", [128, 1], F32, n)

    def ln_stage1(self, r, rB):
        st, stB = self.ln_st.next()
        mv, mvB = self.ln_mv.next()
        rs, rsB = self.ln_rs.next()
        P = self.P
        for j in range(2):
            P.op("dve", lambda e, j=j: e.bn_stats(st[:, j, :], r[:, j * 512:(j + 1) * 512]), [rB], [stB])
        P.op("dve", lambda e: e.bn_aggr(mv[:], st[:]), [stB], [mvB])
        self.act(rs[:], mv[:, 1:2], AF.Sqrt, [mvB], [rsB], bias=EPS, scale=1.0)
        return mv, mvB, rs, rsB

    def ln_stage2(self, r, rB, mv, mvB, rs, rsB, wbc, wB, bbc, bB):
        self.recip(rs[:], rs[:], [rsB], [rsB])
        self.stt(r[:], r[:], mv[:, 0:1], wbc[:], ALU.subtract, ALU.mult, [rB, mvB, wB], [rB])
        self.stt(r[:], r[:], rs[:, 0:1], bbc[:], ALU.mult, ALU.add, [rB, rsB, bB], [rB])

    def ln_tile(self, r, rB, wbc, wB, bbc, bB):
        mv, mvB, rs, rsB = self.ln_stage1(r, rB)
        self.ln_stage2(r, rB, mv, mvB, rs, rsB, wbc, wB, bbc, bB)

    def to_hT(self, r, rB, hT, hTB, col0):
        hb, hbB = self.hb_rot.next()
        self.cp("act", hb[:], r[:], [rB], [hbB])
        self.hb_to_hT(hb, hbB, hT, hTB, col0)

    def hb_to_hT(self, hb, hbB, hT, hTB, col0):
        pb, pbB = self.pb()
        for kc in range(8):
            self.tr(pb[:, kc, :], hb[:, kc * 128:(kc + 1) * 128], [hbB], [pbB])
        self.cp("act", hT[:, :, col0:col0 + 128], pb[:, :, :], [pbB], [hTB])

    def load_hT(self, h_dram, ntiles, hT, hTBs, tile0=0):
        for i in range(ntiles):
            hb, hbB = self.hb_rot.next()
            self.dma("pool", hb[:], h_dram[i * 128:(i + 1) * 128, :], w=[hbB])
            self.hb_to_hT(hb, hbB, hT, hTBs[tile0 + i], (tile0 + i) * 128)

    def load_w(self, name, w_dram, c0, c1, kchunks=8, q="pool"):
        n = c1 - c0
        t = self.alloc([128, kchunks, n], BF16)
        bs = [Buf("%s_%d" % (name, k)) for k in range(kchunks)]
        for kc in range(kchunks):
            self.dma(q, t[:, kc, :], w_dram[kc * 128:(kc + 1) * 128, c0:c1], w=[bs[kc]])
        return t, bs

    def tail_fill(self, r, rB, cg, p, pB, bias_bc, biasB):
        self.tt("dve", r[:, cg * 512:(cg + 1) * 512], p[:, :], bias_bc[:, cg * 512:(cg + 1) * 512], ALU.add, [pB, biasB], [rB])

    def tail_finish(self, r, rB, hsrc, ti, lnw, lnwB, lnb, lnbB, hdst, hT=None, hTB=None):
        hh, hhB = self.h_rot.next()
        self.dma("sp", hh[:], hsrc[ti * 128:(ti + 1) * 128, :], w=[hhB])
        self.stt(r[:], hh[:], ALPHA, r[:], ALU.mult, ALU.add, [hhB, rB], [rB])
        mv, mvB, rs, rsB = self.ln_stage1(r, rB)
        self.tail_flush()

        def stage2():
            self.ln_stage2(r, rB, mv, mvB, rs, rsB, lnw, lnwB, lnb, lnbB)
            self.dma("sp", hdst[ti * 128:(ti + 1) * 128, :], r[:], r=[rB])
            if hT is not None:
                self.to_hT(r, rB, hT, hTB, ti * 128)

        self._deferred = stage2

    def tail_flush(self):
        d = getattr(self, "_deferred", None)
        self._deferred = None
        if d is not None:
            d()

    def res_ln_tail(self, pss, bias_bc, biasB, hsrc, ti, lnw, lnwB, lnb, lnbB, hdst, hT=None, hTB=None):
        r, rB = self.r_rot.next()
        for cg in range(2):
            self.tail_fill(r, rB, cg, pss[cg][0], pss[cg][1], bias_bc, biasB)
        self.tail_finish(r, rB, hsrc, ti, lnw, lnwB, lnb, lnbB, hdst, hT, hTB)

    def tail_setup(self, nr=2, nh=2):
        k = self
        self.ln_setup(max(nr, 2))
        self.r_rot = Rot(k, "rres", [128, 1024], F32, nr)
        self.h_rot = Rot(k, "hres", [128, 1024], F32, nh)

    def finish(self):
        self.P.finish()
        self.P.emit()


class IO:
    def __init__(self, k, m=None):
        self.k, self.m = k, m

    def i(self, name, shape):
        return self.m[name] if self.m is not None else self.k.inp(name, shape)

    def o(self, name, shape):
        return self.m[name] if self.m is not None else self.k.out(name, shape)


def build_pre(k=None, io=None):
    solo = k is None
    if solo:
        nc = bass.Bass("TRN2", target_bir_lowering=False)
        k = K(nc)
    io = IO(k, io)
    x = io.i("x", [NT, D])
    lw = io.i("lnw", [1, D])
    lb = io.i("lnb", [1, D])
    h = io.o("h", [NT, D])
    wbc, wB = k.bc_load("wbc", lw[0:1, :], D)
    bbc, bB = k.bc_load("bbc", lb[0:1, :], D)
    k.ln_setup(4)
    rr = Rot(k, "r", [128, 1024], F32, 6)
    for i in range(NTL):
        r, rB = rr.next()
        k.dma("sp", r[:], x[i * 128:(i + 1) * 128, :], w=[rB])
        k.ln_tile(r, rB, wbc, wB, bbc, bB)
        k.dma("sp", h[i * 128:(i + 1) * 128, :], r[:], r=[rB])
    if solo:
        k.finish()
        return nc, k


def ml_inproj(k, h_dram, w_main, w_gate, bfm, bfmB, brow, need_mo, ext_hT=None):
    nc = k.nc
    qT, qTB = k.sb("qT", [128, 4, NT], BF16)
    kT, kTB = k.sb("kT", [128, 4, NT], BF16)
    ktok, ktokB = k.sb("ktok", [128, NTL, 512], BF16)
    vaug, vaugB = k.sb("vaug", [128, NTL, 4, 130], BF16)
    gates, gatesB = k.sb("gates", [128, NTL, 16], F32)
    mo = moB = None
    if need_mo:
        mo, moB = k.sb("mo", [128, NTL, 512], BF16)
    k.push()
    if ext_hT is not None:
        hT, hTB = ext_hT
    else:
        k.hb_rot = Rot(k, "hb", [128, 1024], BF16, 3)
        hT, _ = k.sb("hT", [128, 8, NT], BF16)
        hTB = [Buf("hT%d" % i) for i in range(NTL)]
        k.load_hT(h_dram, NTL, hT, hTB)
    wml, wmlB = k.load_w("wml", w_main, 0, 2048)
    wg, wgB = k.load_w("wg", w_gate, 0, 16)
    bq, bqB = k.sb("bq_s", [128, 4], F32)
    k.ts("pool", bq[:], bfm[:, 0:4], QS, None, ALU.mult, None, [bfmB], [bqB])
    bck, bckB = k.bc_load("bc_k", brow[0:1, 512:1024], 512)
    bcv, bcvB = k.bc_load("bc_v", brow[0:1, 1024:1536], 512)
    bcg, bcgB = k.bc_load("bc_g", brow[0:1, 3584:3600], 16)
    qTBs = [Buf("qT%d" % b) for b in range(4)]
    kTBs = [Buf("kT%d" % b) for b in range(4)]
    tokBs = [Buf("tok%d" % i) for i in range(NTL)]
    k.memset("pool", vaug[:, :, :, 128:130], 1.0, tokBs)
    if need_mo:
        bcm, bcmB = k.bc_load("bc_mo", brow[0:1, 1536:2048], 512)
    for blk in range(4):
        cs = slice(blk * 512, (blk + 1) * 512)
        hr = hTB[blk * 4:(blk + 1) * 4]
        for c in range(8):
            p, pB = k.ps()
            for kc in range(8):
                k.mm(p[:, :], wml[:, kc, c * 128:(c + 1) * 128], hT[:, kc, cs], kc == 0, kc == 7, [wmlB[kc]] + hr, [pB])
            if c < 4:
                k.act(qT[:, c, cs], p[:, :], AF.Identity, [pB, bqB], [qTBs[blk]], bias=bq[:, c:c + 1], scale=QS)
            else:
                k.act(kT[:, c - 4, cs], p[:, :], AF.Identity, [pB, bfmB], [kTBs[blk]], bias=bfm[:, c:c + 1], scale=1.0)
        for st in range(4):
            ti = blk * 4 + st
            ts_ = slice(ti * 128, (ti + 1) * 128)
            p, pB = k.ps()
            for kc in range(8):
                k.mm(p[:, :], hT[:, kc, ts_], wml[:, kc, 512:1024], kc == 0, kc == 7, [wmlB[kc], hTB[ti]], [pB])
            k.tt("dve", ktok[:, ti, :], p[:, :], bck[:], ALU.add, [pB, bckB], [tokBs[ti]])
            p, pB = k.ps()
            for kc in range(8):
                k.mm(p[:, :], hT[:, kc, ts_], wml[:, kc, 1024:1536], kc == 0, kc == 7, [wmlB[kc], hTB[ti]], [pB])
            for hd in range(4):
                k.tt("dve", vaug[:, ti, hd, 0:128], p[:, hd * 128:(hd + 1) * 128], bcv[:, hd * 128:(hd + 1) * 128], ALU.add,
                     [pB, bcvB], [tokBs[ti]])
            if need_mo:
                p, pB = k.ps()
                for kc in range(8):
                    k.mm(p[:, :], hT[:, kc, ts_], wml[:, kc, 1536:2048], kc == 0, kc == 7, [wmlB[kc], hTB[ti]], [pB])
                k.tt("dve", mo[:, ti, :], p[:, :], bcm[:], ALU.add, [pB, bcmB], [tokBs[ti]])
            p, pB = k.ps()
            for kc in range(8):
                k.mm(p[:, 0:16], hT[:, kc, ts_], wg[:, kc, :], kc == 0, kc == 7, [wgB[kc], hTB[ti]], [pB])
            k.tt("dve", gates[:, ti, :], p[:, 0:16], bcg[:], ALU.add, [pB, bcgB], [tokBs[ti]])
    k.pop()
    return dict(qT=qT, kT=kT, ktok=ktok, vaug=vaug, gates=gates, mo=mo, qTBs=qTBs, kTBs=kTBs, tokBs=tokBs)


def ml_scan(k, m, sset, cinit, emit_out, emit_state):
    nc = k.nc
    qT, kT, ktok, vaug, gates = m["qT"], m["kT"], m["ktok"], m["vaug"], m["gates"]
    tokBs = m["tokBs"]
    gi = gates[:, :, 8 * sset:8 * sset + 4]
    gf = gates[:, :, 8 * sset + 4:8 * sset + 8]
    tri = k.cf[:, 1 + sset, :]
    sfx = str(sset)
    nlf, nlfB = k.sb("nlf" + sfx, [128, NTL, 4], F32)
    tmp, tmpB = k.sb("gtmp" + sfx, [128, NTL, 4], F32)
    ee, eeB = k.sb("ee" + sfx, [128, NTL, 4], F32)
    ena, enaB = k.sb("ena" + sfx, [128, NTL, 4], F32)
    lam, lamB = k.sb("lam" + sfx, [128, NTL, 4], F32)
    k.act(tmp[:], gf, AF.Exp, tokBs, [tmpB], scale=-1.0)
    k.act(nlf[:], tmp[:], AF.Ln, [tmpB], [nlfB], bias=1.0)
    pA, pAB = k.psf[0], k.psfB[0]
    nlf2 = nlf[:].rearrange("p a b -> p (a b)")
    k.mm(pA[:, 0:64], tri, nlf2, True, True, [k.cfB, nlfB], [pAB])
    pE, pEB = pA, pAB
    k.mm(pE[:, 64:128], k.cf[:, 3, :], nlf2, True, True, [k.cfB, nlfB], [pEB])
    pA3 = pA[:, 0:64].rearrange("p (a b) -> p a b", b=4)
    pE3 = pE[:, 64:128].rearrange("p (a b) -> p a b", b=4)
    k.tt("dve", tmp[:], gi, pA3, ALU.add, tokBs + [pAB], [tmpB])
    k.act(ee[:], tmp[:], AF.Exp, [tmpB], [eeB])
    k.act(ena[:], pA3, AF.Exp, [pAB], [enaB])
    k.act(lam[:], pE3, AF.Exp, [pEB], [lamB], scale=-1.0)
    gB = [nlfB, eeB, enaB, lamB]

    Dst = [k.sb("D%s_%d" % (sfx, hd), [128, 129], F32) for hd in range(4)]
    Cbf = [Rot(k, "Cbf%s_%d" % (sfx, hd), [128, 130], BF16, 2) for hd in range(4)]
    ev_rot = Rot(k, "ev" + sfx, [128, 130], BF16, 6)
    wt_rot = Rot(k, "wt" + sfx, [128, 128], BF16, 6)
    t1_rot = Rot(k, "t1" + sfx, [128, 4], F32, 4)
    order = list(range(NTL)) if sset == 0 else list(range(NTL - 1, -1, -1))
    cur = [None] * 4
    if cinit is not None:
        ci, ciB = cinit
        for hd in range(4):
            cb, cbB = Cbf[hd].next()
            k.cp("act", cb[:, 0:129], ci[:, hd, :], [ciB], [cbB])
            cur[hd] = (cb, cbB)
    items = [(n, c, hd) for n, c in enumerate(order) for hd in range(4)]
    ctx = {}

    def stage_p(it):
        n, c, hd = it
        cs = slice(c * 128, (c + 1) * 128)
        blk = c // 4
        ev, evB = ev_rot.next()
        k.act(ev[:, 0:129], vaug[:, c, hd, 0:129], AF.Identity, [tokBs[c], eeB], [evB], scale=ee[:, c, hd:hd + 1])
        pk, pkB, bkB = k.packed("scan", (1, 2, 3, 4, 5), 3)
        k.mm(pk[:, 0:128], kT[:, hd, cs], qT[:, hd, cs], True, True, [m["kTBs"][blk], m["qTBs"][blk]], [pkB[0], bkB])
        wt, wtB = wt_rot.next()
        k.tt("dve", wt[:], pk[:, 0:128], tri, ALU.mult, [pkB[0], bkB, k.cfB], [wtB])
        ctx[it] = (ev, evB, pk, pkB, wt, wtB, bkB)

    def stage_m(it):
        n, c, hd = it
        cs = slice(c * 128, (c + 1) * 128)
        blk = c // 4
        ev, evB, pk, pkB, wt, wtB, bkB = ctx[it]
        prev = order[n - 1] if n > 0 else None
        pO, pOB = pk[:, 128:257], pkB[1]
        has_c = cur[hd] is not None
        k.mm(pO[:, 0:129], wt[:], ev[:, 0:129], True, not has_c, [wtB, evB], [pOB, bkB])
        if has_c:
            cb, cbB = cur[hd]
            k.mm(pO[:, 0:129], qT[:, hd, cs], cb[:, 0:129], False, True, [m["qTBs"][blk], cbB], [pOB, bkB])
        pG, pGB = pk[:, 257:386], pkB[2]
        k.mm(pG[:, 0:129], ktok[:, c, hd * 128:(hd + 1) * 128], ev[:, 0:129], True, True, [tokBs[c], evB], [pGB, bkB])
        Dt, DB = Dst[hd]
        if n == 0:
            if cinit is None:
                k.cp("dve", Dt[:], pG[:, 0:129], [pGB, bkB], [DB])
            else:
                k.tt("dve", Dt[:], pG[:, 0:129], cinit[0][:, hd, :], ALU.add, [pGB, bkB, cinit[1]], [DB])
        else:
            k.stt(Dt[:], Dt[:], lam[:, prev, hd:hd + 1], pG[:, 0:129], ALU.mult, ALU.add, [DB, lamB, pGB, bkB], [DB])
        if n < NTL - 1:
            cb, cbB = Cbf[hd].next()
            k.act(cb[:, 0:129], Dt[:], AF.Identity, [DB, lamB], [cbB], scale=lam[:, c, hd:hd + 1])
            cur[hd] = (cb, cbB)
        if hd == 0:
            ctx["t1"] = t1_rot.next()
        t1, t1B = ctx["t1"]
        k.act(t1[:, hd:hd + 1], pO[:, 128:129], AF.Abs, [pOB, bkB], [t1B])
        ctx[it] = (pO, (pOB, bkB), t1, t1B)

    def stage_e(n, c):
        t1, t1B = ctx[(n, c, 0)][2:4]
        k.tt("dve", t1[:], t1[:], ena[:, c, :], ALU.max, [t1B, enaB], [t1B])
        k.recip(t1[:], t1[:], [t1B], [t1B])
        for hd in range(4):
            pO, pOB, _, _ = ctx.pop((n, c, hd))
            emit_out(c, hd, pO, pOB, t1[:, hd:hd + 1], t1B)

    stage_p(items[0])
    for i, it in enumerate(items):
        if i + 1 < len(items):
            stage_p(items[i + 1])
        stage_m(it)
        if it[2] == 3:
            stage_e(it[0], it[1])
    prev = order[-1]
    k.nrot = 5
    if emit_state is not None:
        for hd in range(4):
            emit_state(hd, Dst[hd][0], Dst[hd][1], lam[:, prev, hd:hd + 1], lamB)


NA_KTS = {0: [0, 1, 2, 3], 1: [0, 1, 2, 3], 14: [12, 13, 14, 15, 16], 15: [13, 14, 15, 16, 17]}
NA_OFF = {0: 5, 1: 9, 14: 13, 15: 18}


def build_a1(k=None, io=None):
    solo = k is None
    if solo:
        nc = bass.Bass("TRN2", target_bir_lowering=False)
        k = K(nc)
    io = IO(k, io)
    h = io.i("h", [NT, D])
    hh = io.i("hh", [256, D])
    w_main = io.i("w_main", [D, 3584])
    bfm_d = io.i("bfm", [128, 28])
    brow = io.i("brow", [1, 3600])
    rpbt = io.i("rpbt", [8, 128, 23, 128])
    y_na = io.o("y_na", [NT, 512])
    if solo:
        k.load_consts()
    k.hb_rot = Rot(k, "hb", [128, 1024], BF16, 3)
    if io.m is not None and "hT" in io.m:
        hT, hTB = io.m["hT"]
    else:
        hT, _ = k.sb("hT", [128, 8, NT + 256], BF16)
        hTB = [Buf("hT%d" % i) for i in range(18)]
    bfm, bfmB = k.sb("bfm", [128, 28], F32)
    k.dma("sp", bfm[:], bfm_d[:, :], w=[bfmB])
    k.load_hT(h, NTL, hT, hTB)
    k.load_hT(hh, 2, hT, hTB, tile0=16)

    wna, wnaB = k.load_w("wna", w_main, 2048, 3584)
    bnq, bnqB = k.sb("bnq", [128, 4], F32)
    k.ts("pool", bnq[:], bfm[:, 16:20], 0.125, None, ALU.mult, None, [bfmB], [bnqB])
    bcnv, bcnvB = k.bc_load("bc_nv", brow[0:1, 3072:3584], 512)
    nqT, _ = k.sb("nqT", [128, 4, NT], BF16)
    nkT, _ = k.sb("nkT", [128, 4, NT + 256], BF16)
    nva, _ = k.sb("nva", [128, 18, 8, 66], BF16)
    nqB = [Buf("nq%d" % b) for b in range(4)]
    nkB = [Buf("nk%d" % b) for b in range(5)]
    nvB = [Buf("nv%d" % i) for i in range(18)]
    k.memset("pool", nva[:, :, :, 64:66], 1.0, nvB)
    for blk in range(5):
        n = 512 if blk < 4 else 256
        cs = slice(blk * 512, blk * 512 + n)
        hr = hTB[blk * 4:blk * 4 + n // 128]
        for hp in range(4):
            if blk < 4:
                p, pB = k.ps()
                for kc in range(8):
                    k.mm(p[:, 0:n], wna[:, kc, hp * 128:(hp + 1) * 128], hT[:, kc, cs], kc == 0, kc == 7, [wnaB[kc]] + hr, [pB])
                k.act(nqT[:, hp, cs], p[:, 0:n], AF.Identity, [pB, bnqB], [nqB[blk]], bias=bnq[:, hp:hp + 1], scale=0.125)
            p, pB = k.ps()
            for kc in range(8):
                k.mm(p[:, 0:n], wna[:, kc, 512 + hp * 128:512 + (hp + 1) * 128], hT[:, kc, cs], kc == 0, kc == 7, [wnaB[kc]] + hr, [pB])
            k.act(nkT[:, hp, cs], p[:, 0:n], AF.Identity, [pB, bfmB], [nkB[blk]], bias=bfm[:, 20 + hp:21 + hp], scale=1.0)
    for ti in range(18):
        p, pB = k.ps()
        for kc in range(8):
            k.mm(p[:, :], hT[:, kc, ti * 128:(ti + 1) * 128], wna[:, kc, 1024:1536], kc == 0, kc == 7, [wnaB[kc], hTB[ti]], [pB])
        k.tt("dve", nva[:, ti, :, 0:64], p[:, :].rearrange("p (a b) -> p a b", b=64),
             bcnv[:].rearrange("p (a b) -> p a b", b=64), ALU.add, [pB, bcnvB], [nvB[ti]])

    k.P.barrier()
    k.nrot = 4
    ytile = Rot(k, "yna", [128, 512], F32, 2)
    bt_rot = Rot(k, "btile", [128, 23, 128], BF16, 2)
    pT_rot = Rot(k, "pT", [128, 5, 128], BF16, 4)
    rd_rot = Rot(k, "rd", [128, 1], F32, 8)
    ysb, ysbB = k.sb("ysb", [128, NTL, 512], F32)
    yBs = [Buf("y%d" % i) for i in range(NTL)]
    naslot = [0]
    naslotB = [Buf("naslot%d" % i) for i in range(14)]
    na_items = [(hd, qp) for hd in range(8) for qp in range(NTL)]
    nactx = {}
    btcur = {}

    def na_s(it):
        hd, qp = it
        if qp == 0:
            bt, btB = bt_rot.next()
            k.dma("pool", bt[:], rpbt[hd, :, :, :], w=[btB])
            btcur[hd] = (bt, btB)
        bt, btB = btcur[hd]
        hp, r0 = hd // 2, 64 * (hd % 2)
        kts = NA_KTS.get(qp, [qp - 2, qp - 1, qp, qp + 1, qp + 2])
        off = NA_OFF.get(qp, 0)
        qs = slice(qp * 128, (qp + 1) * 128)
        p1, p1B = k.ps()
        p2, p2B = k.ps()
        for i, kt in enumerate(kts):
            pp, ppB = (p1, p1B) if i < 4 else (p2, p2B)
            o = (i % 4) * 128
            kblk = min(kt // 4, 4)
            k.mm(pp[:, o:o + 128], nkT[r0:r0 + 64, hp, kt * 128:(kt + 1) * 128], nqT[r0:r0 + 64, hp, qs], True, False,
                 [nkB[kblk], nqB[qp // 4]], [ppB])
            k.mm(pp[:, o:o + 128], k.ident[:], bt[:, off + i, :], False, True, [k.identB, btB], [ppB])
        pT, pTB = pT_rot.next()
        n1 = min(len(kts), 4)
        k.act(pT[:, 0:n1, :], p1[:, 0:n1 * 128].rearrange("p (a b) -> p a b", b=128), AF.Exp, [p1B], [pTB])
        if len(kts) == 5:
            k.act(pT[:, 4, :], p2[:, 0:128], AF.Exp, [p2B], [pTB])
        nactx[it] = (kts, pT, pTB)

    def na_v(it):
        hd, qp = it
        kts, pT, pTB = nactx.pop(it)
        slot = naslot[0] % 14
        naslot[0] += 1
        bank = 4 + slot % 2
        pO = k.psf[bank][:, (slot // 2) * 66:(slot // 2) * 66 + 66]
        pOB = naslotB[slot]
        bkB = k.psfB[bank]
        for i, kt in enumerate(kts):
            k.mm(pO[:, 0:65], pT[:, i, :], nva[:, kt, hd, 0:65], i == 0, i == len(kts) - 1, [pTB, nvB[kt]], [pOB, bkB])
        rd, rdB = rd_rot.next()
        k.recip(rd[:], pO[:, 64:65], [pOB, bkB], [rdB])
        k.ts("dve", ysb[:, qp, hd * 64:(hd + 1) * 64], pO[:, 0:64], rd[:, 0:1], None, ALU.mult, None, [pOB, bkB, rdB], [yBs[qp]])

    na_s(na_items[0])
    for i, it in enumerate(na_items):
        if i + 1 < len(na_items):
            na_s(na_items[i + 1])
        na_v(it)
    k.nrot = 5
    for qp in range(NTL):
        k.dma("sp", y_na[qp * 128:(qp + 1) * 128, :], ysb[:, qp, :], r=[yBs[qp]])

    if solo:
        k.finish()
        return nc, k


def build_a2(k=None, io=None):
    solo = k is None
    if solo:
        nc = bass.Bass("TRN2", target_bir_lowering=False)
        k = K(nc)
    io = IO(k, io)
    h = io.i("h", [NT, D])
    w_main = io.i("w_main", [D, 3584])
    w_gate = io.i("w_gate", [D, 16])
    bfm_d = io.i("bfm", [128, 28])
    brow = io.i("brow", [1, 3600])
    hd1_o = io.o("hd1", [NT, 512])
    st_o = io.o("st1", [128, 4, 129])
    mo_o = io.o("mo", [NT, 512])
    if solo:
        k.load_consts()
    bfm, bfmB = k.sb("bfm", [128, 28], F32)
    k.dma("sp", bfm[:], bfm_d[:, :], w=[bfmB])
    m = ml_inproj(k, h, w_main, w_gate, bfm, bfmB, brow, need_mo=True)
    hd_rot = Rot(k, "hdt", [128, 512], F32, 3)
    state = {}

    def emit_out(c, hd, pO, pOB, t1, t1B):
        if hd == 0:
            state["t"] = hd_rot.next()
        t, tB = state["t"]
        k.ts("dve", t[:, hd * 128:(hd + 1) * 128], pO[:, 0:128], t1[:, 0:1], None, ALU.mult, None, list(pOB) + [t1B], [tB])
        if hd == 3:
            k.dma("sp", hd1_o[c * 128:(c + 1) * 128, :], t[:], r=[tB])

    sto, stoB = k.sb("sto", [128, 4, 129], F32)

    def emit_state(hd, Dt, DB, lam_ap, lamB):
        k.act(sto[:, hd, :], Dt[:], AF.Identity, [DB, lamB], [stoB], scale=lam_ap)
        if hd == 3:
            k.dma("sp", st_o[:, :, :], sto[:], r=[stoB])

    ml_scan(k, m, 0, None, emit_out, emit_state)
    for ti in range(NTL):
        k.dma("pool", mo_o[ti * 128:(ti + 1) * 128, :], m["mo"][:, ti, :], r=[m["tokBs"][ti]])
    if solo:
        k.finish()
        return nc, k


def build_b1(k=None, io=None):
    solo = k is None
    if solo:
        nc = bass.Bass("TRN2", target_bir_lowering=False)
        k = K(nc)
    io = IO(k, io)
    h = io.i("h", [NT, D])
    w_main = io.i("w_main", [D, 3584])
    w_gate = io.i("w_gate", [D, 16])
    bfm_d = io.i("bfm", [128, 28])
    brow = io.i("brow", [1, 3600])
    hd1 = io.i("hd1", [NT, 512])
    st_in = io.i("st_in", [128, 4, 129])
    hm_o = io.o("hm", [NT, 512])
    if solo:
        k.load_consts()
    bfm, bfmB = k.sb("bfm", [128, 28], F32)
    k.dma("sp", bfm[:], bfm_d[:, :], w=[bfmB])
    m = ml_inproj(k, h, w_main, w_gate, bfm, bfmB, brow, need_mo=False)
    ci, ciB = k.sb("cinit", [128, 4, 129], F32)
    k.dma("sp", ci[:], st_in[:, :, :], w=[ciB])
    hm_rot = Rot(k, "hm_", [128, 512], F32, 3)
    state = {}

    def emit_out(c, hd, pO, pOB, t1, t1B):
        if hd == 0:
            t, tB = hm_rot.next()
            state["t"] = (t, tB)
            k.dma("sp", t[:], hd1[c * 128:(c + 1) * 128, :], w=[tB])
        t, tB = state["t"]
        hs = slice(hd * 128, (hd + 1) * 128)
        k.stt(t[:, hs], pO[:, 0:128], t1[:, 0:1], t[:, hs], ALU.mult, ALU.add, list(pOB) + [t1B, tB], [tB])
        if hd == 3:
            k.dma("sp", hm_o[c * 128:(c + 1) * 128, :], t[:], r=[tB])

    ml_scan(k, m, 1, (ci, ciB), emit_out, None)
    if solo:
        k.finish()
        return nc, k


def build_b1b(k=None, io=None):
    solo = k is None
    if solo:
        nc = bass.Bass("TRN2", target_bir_lowering=False)
        k = K(nc)
    io = IO(k, io)
    h = io.i("h", [NT, D])
    hm = io.i("hm", [NT, 512])
    mo = io.i("mo", [NT, 512])
    y_na = io.i("y_na", [NT, 512])
    mlw = io.i("mlw", [1, 512])
    w_out = io.i("w_out", [D, D])
    vecs = io.i("vecs", [3, D])
    h1_o = io.o("h1", [NT, D])
    if solo:
        k.load_consts()
    k.hb_rot = Rot(k, "hb", [128, 1024], BF16, 2)
    k.tail_setup(nr=4, nh=4)
    vb = [k.bc_load("vec%d" % i, vecs[i:i + 1, :], D) for i in range(3)]
    mlwbc, mlwB = k.bc_load("mlwbc", mlw[0:1, :], 512)
    wo, woB = k.load_w("wo", w_out, 0, 1024)
    hm_rot = Rot(k, "hm_", [128, 512], F32, 4)
    mo_rot = Rot(k, "mot_", [128, 512], F32, 4)
    yn_rot = Rot(k, "ynl_", [128, 512], F32, 4)
    yt_rot = Rot(k, "ytk_", [128, 1024], BF16, 4)
    st4_rot = Rot(k, "st4_", [128, 4, 6], F32, 4)
    mv4_rot = Rot(k, "mv4_", [128, 4, 2], F32, 4)
    rs4_rot = Rot(k, "rs4_", [128, 4], F32, 4)
    yT_rot = Rot(k, "yT", [128, 8, 128], BF16, 4)
    msg_rot = Rot(k, "msg", [128, 512], F32, 4)
    ctx = {}

    def st_a1(ti):
        rows = slice(ti * 128, (ti + 1) * 128)
        t, tB = hm_rot.next()
        k.dma("sp", t[:], hm[rows, :], w=[tB])
        mt, mtB = mo_rot.next()
        k.dma("sp", mt[:], mo[rows, :], w=[mtB])
        yn, ynB = yn_rot.next()
        k.dma("sp", yn[:], y_na[rows, :], w=[ynB])
        yt, ytB = yt_rot.next()
        k.cp("pool", yt[:, 512:1024], yn[:], [ynB], [ytB])
        st4, st4B = st4_rot.next()
        mv4, mv4B = mv4_rot.next()
        rs4, rs4B = rs4_rot.next()
        for j in range(4):
            k.P.op("dve", lambda e, j=j, st4=st4, t=t: e.bn_stats(st4[:, j, :], t[:, j * 128:(j + 1) * 128]), [tB], [st4B])
        for j in range(4):
            k.P.op("dve", lambda e, j=j, st4=st4, mv4=mv4: e.bn_aggr(mv4[:, j, :], st4[:, j, :]), [st4B], [mv4B])
        k.act(rs4[:], mv4[:, :, 1], AF.Sqrt, [mv4B], [rs4B], bias=EPS, scale=1.0)
        k.act(mt[:], mt[:], AF.Sigmoid, [mtB], [mtB])
        msg, msgB = msg_rot.next()
        k.tt("pool", msg[:], mt[:], mlwbc[:], ALU.mult, [mtB, mlwB], [msgB])
        ctx[ti] = (t, tB, yt, ytB, mv4, mv4B, rs4, rs4B, msg, msgB)

    def st_a2(ti):
        t, tB, yt, ytB, mv4, mv4B, rs4, rs4B, msg, msgB = ctx[ti]
        k.recip(rs4[:], rs4[:], [rs4B], [rs4B])
        for j in range(4):
            js = slice(j * 128, (j + 1) * 128)
            k.ts("dve", t[:, js], t[:, js], mv4[:, j, 0:1], rs4[:, j:j + 1], ALU.subtract, ALU.mult, [tB, mv4B, rs4B], [tB])
        k.tt("dve", yt[:, 0:512], t[:], msg[:], ALU.mult, [tB, msgB], [ytB])

    def st_b(ti):
        t, tB, yt, ytB = ctx.pop(ti)[0:4]
        yT, yTB = yT_rot.next()
        pb, pbB = k.pb()
        for kc in range(8):
            k.tr(pb[:, kc, :], yt[:, kc * 128:(kc + 1) * 128], [ytB], [pbB])
        k.cp("act", yT[:], pb[:, :, :], [pbB], [yTB])
        pss = []
        for cg in range(2):
            p, pB = k.ps()
            for kc in range(8):
                k.mm(p[:, :], yT[:, kc, :], wo[:, kc, cg * 512:(cg + 1) * 512], kc == 0, kc == 7, [yTB, woB[kc]], [pB])
            pss.append((p, pB))
        if io.m is not None and "h1T" in io.m:
            k.res_ln_tail(pss, vb[0][0], vb[0][1], h, ti, vb[1][0], vb[1][1], vb[2][0], vb[2][1], h1_o, io.m["h1T"][0], io.m["h1T"][1][ti])
        else:
            k.res_ln_tail(pss, vb[0][0], vb[0][1], h, ti, vb[1][0], vb[1][1], vb[2][0], vb[2][1], h1_o)

    for step in range(NTL + 2):
        if step < NTL:
            st_a1(step)
        if 1 <= step <= NTL:
            st_a2(step - 1)
        if step >= 2:
            st_b(step - 2)
    k.tail_flush()
    if solo:
        k.finish()
        return nc, k


def build_b2(k=None, io=None):
    solo = k is None
    if solo:
        nc = bass.Bass("TRN2", target_bir_lowering=False)
        k = K(nc)
    io = IO(k, io)
    h1_o = io.i("h1", [NT, D])
    vecs = io.i("vecs", [3, D])
    mem = io.i("mem", [256, D])
    w_xq = io.i("w_xq", [D, D])
    w_xkv = io.i("w_xkv", [D, 2 * D])
    w_xo = io.i("w_xo", [D, D])
    h2_o = io.o("h2", [NT, D])
    if solo:
        k.load_consts()
    k.hb_rot = Rot(k, "hb", [128, 1024], BF16, 2)
    k.tail_setup(nr=3)
    if io.m is not None and "h1T" in io.m:
        hT, hTB = io.m["h1T"]
    else:
        hT, _ = k.sb("hT", [128, 8, NT], BF16)
        hTB = [Buf("hT%d" % i) for i in range(NTL)]
        k.load_hT(h1_o, NTL, hT, hTB)
    vb = [None, None, None] + [k.bc_load("vec%d" % i, vecs[i:i + 1, :], D) for i in range(3)]
    yT_rot = Rot(k, "yT", [128, 8, 128], BF16, 2)
    wq, wqB = k.load_w("wxq", w_xq, 0, 1024)
    wkv, wkvB = k.load_w("wxkv", w_xkv, 0, 2048)
    wxo, wxoB = k.load_w("wxo", w_xo, 0, 1024)
    memT, memTB = k.sb("memT", [128, 8, 256], BF16)
    memTBs = [memTB, Buf("memT1")]
    k.load_hT(mem, 2, memT, memTBs)
    kTm, kTmB = k.sb("kTm", [128, 8, 256], BF16)
    vma, vmaB = k.sb("vma", [128, 2, 4, 258], BF16)
    k.memset("pool", vma[:, :, :, 256:258], 1.0, [vmaB])
    for ch in range(8):
        p, pB = k.ps()
        for kc in range(8):
            k.mm(p[:, 0:256], wkv[:, kc, ch * 128:(ch + 1) * 128], memT[:, kc, :], kc == 0, kc == 7, [wkvB[kc]] + memTBs, [pB])
        k.cp("act", kTm[:, ch, :], p[:, 0:256], [pB], [kTmB])
    for mt in range(2):
        for cg in range(2):
            p, pB = k.ps()
            for kc in range(8):
                k.mm(p[:, :], memT[:, kc, mt * 128:(mt + 1) * 128], wkv[:, kc, 1024 + cg * 512:1024 + (cg + 1) * 512], kc == 0, kc == 7,
                     [wkvB[kc]] + memTBs, [pB])
            k.cp("dve", vma[:, mt, 2 * cg:2 * cg + 2, 0:256], p[:, :].rearrange("p (a b) -> p a b", b=256), [pB], [vmaB])
    qx_rot = Rot(k, "qxT", [128, 8, 512], BF16, 2)
    px_rot = Rot(k, "pxT", [128, 2, 512], BF16, 3)
    ox_rot = Rot(k, "oxt", [128, 4, 1024], BF16, 2)
    rd_rot = Rot(k, "rdx", [128, 1], F32, 4)

    def qx_stage(blk):
        cs = slice(blk * 512, (blk + 1) * 512)
        hr = hTB[blk * 4:(blk + 1) * 4]
        qx, qxB = qx_rot.next()
        for ch in range(8):
            p, pB = k.ps()
            for kc in range(8):
                k.mm(p[:, :], wq[:, kc, ch * 128:(ch + 1) * 128], hT[:, kc, cs], kc == 0, kc == 7, [wqB[kc]] + hr, [pB])
            k.act(qx[:, ch, :], p[:, :], AF.Identity, [pB], [qxB], scale=1.0 / 16.0)
        return qx, qxB

    def heads_stage(blk, qx, qxB):
        ox, oxB = ox_rot.next()

        def xa_s(hd):
            px, pxB = px_rot.next()
            for mt in range(2):
                p, pB = k.ps()
                for half in range(2):
                    k.mm(p[:, :], kTm[:, 2 * hd + half, mt * 128:(mt + 1) * 128], qx[:, 2 * hd + half, :], half == 0, half == 1,
                         [kTmB, qxB], [pB])
                k.act(px[:, mt, :], p[:, :], AF.Exp, [pB], [pxB])
            return px, pxB

        def xa_v(hd, px, pxB):
            for st in range(4):
                pO, pOB = k.ps()
                for mt in range(2):
                    k.mm(pO[:, 0:257], px[:, mt, st * 128:(st + 1) * 128], vma[:, mt, hd, 0:257], mt == 0, mt == 1, [pxB, vmaB], [pOB])
                rd, rdB = rd_rot.next()
                k.recip(rd[:], pO[:, 256:257], [pOB], [rdB])
                k.ts("dve", ox[:, st, hd * 256:(hd + 1) * 256], pO[:, 0:256], rd[:, 0:1], None, ALU.mult, None, [pOB, rdB], [oxB])

        nxt = xa_s(0)
        for hd in range(4):
            curp = nxt
            if hd < 3:
                nxt = xa_s(hd + 1)
            xa_v(hd, curp[0], curp[1])
        return ox, oxB

    def out_stage(blk, ox, oxB):
        for st in range(4):
            ti = blk * 4 + st
            yT, yTB = yT_rot.next()
            pb, pbB = k.pb()
            for kc in range(8):
                k.tr(pb[:, kc, :], ox[:, st, kc * 128:(kc + 1) * 128], [oxB], [pbB])
            k.cp("act", yT[:], pb[:, :, :], [pbB], [yTB])
            pss = []
            for cg in range(2):
                p, pB = k.ps()
                for kc in range(8):
                    k.mm(p[:, :], yT[:, kc, :], wxo[:, kc, cg * 512:(cg + 1) * 512], kc == 0, kc == 7, [yTB, wxoB[kc]], [pB])
                pss.append((p, pB))
            k.res_ln_tail(pss, vb[3][0], vb[3][1], h1_o, ti, vb[4][0], vb[4][1], vb[5][0], vb[5][1], h2_o)

    q = qx_stage(0)
    for blk in range(4):
        o = heads_stage(blk, q[0], q[1])
        if blk < 3:
            q = qx_stage(blk + 1)
        out_stage(blk, o[0], o[1])
    k.tail_flush()
    if solo:
        k.finish()
        return nc, k


def build_c(k=None, io=None):
    solo = k is None
    if solo:
        nc = bass.Bass("TRN2", target_bir_lowering=False)
        k = K(nc)
    io = IO(k, io)
    h2 = io.i("h2", [NT, D])
    h2h = io.i("h2h", [128, D])
    w_upr = io.i("w_upr", [44, 128, 8, 128])
    bupf_d = io.i("bupf", [128, 44])
    bupf2_d = io.i("bupf2", [128, 88])
    wdwf_d = io.i("wdwf", [128, 44, 3])
    bdwf_d = io.i("bdwf", [128, 44])
    w_down = io.i("w_down", [DFF, D])
    vecs = io.i("vecs", [3, D])
    h3_o = io.o("h3", [NT, D])
    if solo:
        k.load_consts()
    k.hb_rot = Rot(k, "hb", [128, 1024], BF16, 3)
    k.tail_setup(nr=6, nh=4)
    hT, _ = k.sb("hT", [128, 8, NT + 128], BF16)
    hTB = [Buf("hT%d" % i) for i in range(17)]
    k.load_hT(h2, NTL, hT, hTB)
    k.load_hT(h2h, 1, hT, hTB, tile0=16)
    vb = [k.bc_load("vec%d" % i, vecs[i:i + 1, :], D) for i in range(3)]
    small = {}
    for nm, src, shp in (("bupf", bupf_d, [128, 44]), ("bupf2", bupf2_d, [128, 88]), ("wdwf", wdwf_d, [128, 44, 3]), ("bdwf", bdwf_d, [128, 44])):
        t, b = k.sb(nm, shp, F32)
        k.dma("sp", t[:], src, w=[b])
        small[nm] = (t, b)
    bupf, bupfB = small["bupf"]
    bupf2, bupf2B = small["bupf2"]
    wdwf, wdwfB = small["wdwf"]
    bdwf, bdwfB = small["bdwf"]
    yT, _ = k.sb("yTf", [128, 22, 1024], BF16)
    yTBs = [[Buf("yTf%d_%d" % (i, b)) for b in range(2)] for i in range(22)]
    wu_rot = Rot(k, "wu", [128, 8, 128], BF16, 4)
    x_rot = Rot(k, "xt", [128, 1026], F32, 3)
    a_rot = Rot(k, "at", [128, 1024], F32, 4)
    g_rot = Rot(k, "gt", [128, 1024], F32, 2)
    wd_rot = Rot(k, "wdp", [128, 512], BF16, 6)
    for sb_ in range(2):
        T0 = sb_ * 1024
        pH, pHB = k.psx()
        lo = T0 - 1 if sb_ > 0 else T0
        hi = T0 + 1024
        hr = [hTB[sb_ * 8 + j] for j in range(8)]
        hread = hr + [hTB[(T0 + 1024) // 128]] + ([hTB[(T0 - 1) // 128]] if sb_ > 0 else [])
        for cp_ in range(22):
            av = []
            for part in range(2):
                ch = cp_ + 22 * part
                wu, wuB = wu_rot.next()
                k.dma("pool", wu[:], w_upr[ch, :, :, :], w=[wuB])
                pp = [k.ps(), k.ps()]
                for bi in range(2):
                    cs = slice(T0 + bi * 512, T0 + (bi + 1) * 512)
                    for kc in range(8):
                        k.mm(pp[bi][0][:, :], wu[:, kc, :], hT[:, kc, cs], kc == 0, kc == 7, [wuB] + hr[bi * 4:(bi + 1) * 4], [pp[bi][1]])
                for kc in range(8):
                    k.mm(pH[:, 2 * ch:2 * ch + 1], wu[:, kc, :], hT[:, kc, lo:lo + 1], kc == 0, kc == 7, [wuB] + hread, [pHB])
                for kc in range(8):
                    k.mm(pH[:, 2 * ch + 1:2 * ch + 2], wu[:, kc, :], hT[:, kc, hi:hi + 1], kc == 0, kc == 7, [wuB] + hread, [pHB])
                x, xB = x_rot.next()
                for bi in range(2):
                    k.act(x[:, 1 + bi * 512:513 + bi * 512], pp[bi][0][:, :], AF.Identity, [pp[bi][1], bupfB], [xB],
                          bias=bupf[:, ch:ch + 1], scale=1.0)
                a, aB = a_rot.next()
                k.act(a[:], x[:, 1:1025], AF.Identity, [xB, wdwfB, bdwfB], [aB], bias=bdwf[:, ch:ch + 1], scale=wdwf[:, ch, 1:2])
                k.tt("dve", x[:, 0:1026:1025], pH[:, 2 * ch:2 * ch + 2], bupf2[:, 2 * ch:2 * ch + 2], ALU.add, [pHB, bupf2B], [xB])
                if sb_ == 0:
                    k.memset("dve", x[:, 0:1], 0.0, [xB])
                k.stt(a[:], x[:, 0:1024], wdwf[:, ch, 0:1], a[:], ALU.mult, ALU.add, [xB, wdwfB, aB], [aB])
                k.stt(a[:], x[:, 2:1026], wdwf[:, ch, 2:3], a[:], ALU.mult, ALU.add, [xB, wdwfB, aB], [aB])
                av.append((a, aB))
            g, gB = g_rot.next()
            k.act(g[:], av[0][0][:], AF.Gelu, [av[0][1]], [gB])
            k.tt("dve", yT[:, cp_, :], g[:], av[1][0][:], ALU.mult, [gB, av[1][1]], yTBs[cp_])
        for grp in range(2):
            rts = [k.r_rot.next() for _ in range(4)]
            for cg in range(2):
                pl = [k.ps() for _ in range(4)]
                for cc in range(22):
                    wd, wdB = wd_rot.next()
                    k.dma("pool", wd[:], w_down[cc * 128:(cc + 1) * 128, cg * 512:(cg + 1) * 512], w=[wdB])
                    for s4 in range(4):
                        k.mm(pl[s4][0][:, :], yT[:, cc, grp * 512 + s4 * 128:grp * 512 + (s4 + 1) * 128], wd[:, :], cc == 0, cc == 21,
                             [yTBs[cc][grp], wdB], [pl[s4][1]])
                for s4 in range(4):
                    k.tail_fill(rts[s4][0], rts[s4][1], cg, pl[s4][0], pl[s4][1], vb[0][0], vb[0][1])
            for s4 in range(4):
                ti = sb_ * 8 + grp * 4 + s4
                k.tail_finish(rts[s4][0], rts[s4][1], h2, ti, vb[1][0], vb[1][1], vb[2][0], vb[2][1], h3_o)
    k.tail_flush()
    if solo:
        k.finish()
        return nc, k


PAIRS = [[0, 1], [2, 3], [4, 5], [6, 7]]


def xchg(k, src, dst, srcB, dstB):
    k.P.cc(lambda e: e.collective_compute("AllReduce", ALU.add, replica_groups=PAIRS, ins=[src.opt()], outs=[dst.opt()]),
           [srcB], [dstB], inc=1)


def phase_halo(k, h, hh, cc_src, cc_dst):
    k.push()
    srcB, dstB = Buf("hsrc"), Buf("hdst")
    rot = Rot(k, "hx", [128, 1024], F32, 4)
    mine = []
    for j, r0 in enumerate((1920, 1792)):
        t, tB = rot.next()
        k.dma("sp", t[:], h[r0:r0 + 128, :], w=[tB])
        k.dma("sp", cc_src[j * 128:(j + 1) * 128, :], t[:], r=[tB], w=[srcB])
        mine.append((t, tB))
    xchg(k, cc_src, cc_dst, srcB, dstB)
    for j in range(2):
        u, uB = rot.next()
        k.dma("sp", u[:], cc_dst[j * 128:(j + 1) * 128, :], r=[dstB], w=[uB])
        k.tt("dve", u[:], u[:], mine[j][0][:], ALU.subtract, [uB, mine[j][1]], [uB])
        k.dma("sp", hh[j * 128:(j + 1) * 128, :], u[:], r=[uB])
    k.pop()


def phase_row(k, h2, h2h, cc_src, cc_dst):
    k.push()
    srcB, dstB = Buf("rsrc"), Buf("rdst")
    t, tB = k.sb("rowt", [1, 1024], F32)
    u, uB = k.sb("rowu", [1, 1024], F32)
    k.dma("sp", t[:], h2[2047:2048, :], w=[tB])
    k.dma("sp", cc_src[0:1, :], t[:], r=[tB], w=[srcB])
    xchg(k, cc_src, cc_dst, srcB, dstB)
    k.dma("sp", u[:], cc_dst[0:1, :], r=[dstB], w=[uB])
    k.tt("dve", u[:], u[:], t[:], ALU.subtract, [uB, tB], [uB])
    k.dma("sp", h2h[0:1, :], u[:], r=[uB])
    k.pop()


def phase_B(k, io):
    h, w_main, w_gate, bfm_d, brow = io["h"], io["w_main"], io["w_gate"], io["bfm"], io["brow"]
    hd1, mo_o, hm_o, st_src, st_dst = io["hd1"], io["mo"], io["hm"], io["st_src"], io["st_dst"]
    bfm, bfmB = k.sb("bfm", [128, 28], F32)
    k.dma("sp", bfm[:], bfm_d[:, :], w=[bfmB])
    m = ml_inproj(k, h, w_main, w_gate, bfm, bfmB, brow, need_mo=True, ext_hT=io.get("hT"))
    for ti in range(NTL):
        k.dma("pool", mo_o[ti * 128:(ti + 1) * 128, :], m["mo"][:, ti, :], r=[m["tokBs"][ti]])
    hd_rot = Rot(k, "hdt", [128, 512], F32, 3)
    hd1B = [Buf("hd1_%d" % c) for c in range(NTL)]
    srcB, dstB = Buf("ssrc"), Buf("sdst")
    state = {}

    def emit_out1(c, hd, pO, pOB, t1, t1B):
        if hd == 0:
            state["t"] = hd_rot.next()
        t, tB = state["t"]
        k.ts("dve", t[:, hd * 128:(hd + 1) * 128], pO[:, 0:128], t1[:, 0:1], None, ALU.mult, None, list(pOB) + [t1B], [tB])
        if hd == 3:
            k.dma("sp", hd1[c * 128:(c + 1) * 128, :], t[:], r=[tB], w=[hd1B[c]])

    sto, stoB = k.sb("sto", [128, 4, 129], F32)

    def emit_state(hd, Dt, DB, lam_ap, lamB):
        k.act(sto[:, hd, :], Dt[:], AF.Identity, [DB, lamB], [stoB], scale=lam_ap)
        if hd == 3:
            k.dma("sp", st_src[:, :], sto[:].rearrange("p a b -> p (a b)"), r=[stoB], w=[srcB])

    ml_scan(k, m, 0, None, emit_out1, emit_state)
    xchg(k, st_src, st_dst, srcB, dstB)
    ci, ciB = k.sb("cinit", [128, 4, 129], F32)
    k.dma("sp", ci[:].rearrange("p a b -> p (a b)"), st_dst[:, :], r=[dstB], w=[ciB])
    k.tt("dve", ci[:], ci[:], sto[:], ALU.subtract, [ciB, stoB], [ciB])

    def emit_out2(c, hd, pO, pOB, t1, t1B):
        if hd == 0:
            t, tB = hd_rot.next()
            state["t"] = (t, tB)
            k.dma("sp", t[:], hd1[c * 128:(c + 1) * 128, :], r=[hd1B[c]], w=[tB])
        t, tB = state["t"]
        hs = slice(hd * 128, (hd + 1) * 128)
        k.stt(t[:, hs], pO[:, 0:128], t1[:, 0:1], t[:, hs], ALU.mult, ALU.add, list(pOB) + [t1B, tB], [tB])
        if hd == 3:
            k.dma("sp", hm_o[c * 128:(c + 1) * 128, :], t[:], r=[tB])

    ml_scan(k, m, 1, (ci, ciB), emit_out2, None)


def build_fused(depth=DEPTH, stop=None):
    nc = bass.Bass("TRN2", target_bir_lowering=False)
    k = K(nc)
    k.load_consts()
    x = k.inp("x", [NT, D])
    lnw = k.inp("lnw", [1, D])
    lnb = k.inp("lnb", [1, D])
    mem = k.inp("mem", [256, D])
    shapes = dict(w_main=[D, 3584], w_gate=[D, 16], bfm=[128, 28], brow=[1, 3600], rpbt=[8, 128, 23, 128], mlw=[1, 512],
                  w_out=[D, D], vecs1=[3, D], w_xq=[D, D], w_xkv=[D, 2 * D], w_xo=[D, D], vecs2=[3, D],
                  w_upr=[44, 128, 8, 128], bupf=[128, 44], bupf2=[128, 88], wdwf=[128, 44, 3], bdwf=[128, 44],
                  w_down=[DFF, D], vecs3=[3, D])
    W = [{n: k.inp("%s_%d" % (n, l), shp) for n, shp in shapes.items()} for l in range(depth)]
    out = k.out("out", [NT, D])
    S = k.scratch
    hA, hh = S("s_hA", [NT, D]), S("s_hh", [256, D])
    y_na, hd1, mo, hm = S("s_yna", [NT, 512]), S("s_hd1", [NT, 512]), S("s_mo", [NT, 512]), S("s_hm", [NT, 512])
    h1, h2, h2h = S("s_h1", [NT, D]), S("s_h2", [NT, D]), S("s_h2h", [128, D])
    cc_h_src, cc_h_dst = S("cc_h_src", [256, D]), S("cc_h_dst", [256, D])
    cc_r_src, cc_r_dst = S("cc_r_src", [1, D]), S("cc_r_dst", [1, D])
    cc_s_src, cc_s_dst = S("cc_s_src", [128, 516]), S("cc_s_dst", [128, 516])

    class _Stop(Exception):
        pass

    def chk(name):
        if stop == name:
            raise _Stop()

    try:
        k.push()
        build_pre(k, dict(x=x, lnw=lnw, lnb=lnb, h=hA))
        k.pop()
        phase_halo(k, hA, hh, cc_h_src, cc_h_dst)
        chk("pre")
        for l in range(depth):
            w = W[l]
            k.push()
            hTs = (k.alloc([128, 8, NT + 256], BF16), [Buf("hTs%d_%d" % (l, i)) for i in range(18)])
            k.push()
            build_a1(k, dict(h=hA, hh=hh, w_main=w["w_main"], bfm=w["bfm"], brow=w["brow"], rpbt=w["rpbt"], y_na=y_na, hT=hTs))
            k.pop()
            chk("a1")
            k.push()
            phase_B(k, dict(h=hA, w_main=w["w_main"], w_gate=w["w_gate"], bfm=w["bfm"], brow=w["brow"], hd1=hd1, mo=mo, hm=hm,
                            st_src=cc_s_src, st_dst=cc_s_dst, hT=hTs))
            k.pop()
            k.pop()
            chk("B")
            k.push()
            h1Ts = (k.alloc([128, 8, NT], BF16), [Buf("h1Ts%d_%d" % (l, i)) for i in range(NTL)])
            k.push()
            build_b1b(k, dict(h=hA, hm=hm, mo=mo, y_na=y_na, mlw=w["mlw"], w_out=w["w_out"], vecs=w["vecs1"], h1=h1, h1T=h1Ts))
            k.pop()
            chk("b1b")
            k.push()
            build_b2(k, dict(h1=h1, vecs=w["vecs2"], mem=mem, w_xq=w["w_xq"], w_xkv=w["w_xkv"], w_xo=w["w_xo"], h2=h2, h1T=h1Ts))
            k.pop()
            k.pop()
            phase_row(k, h2, h2h, cc_r_src, cc_r_dst)
            chk("b2")
            k.push()
            build_c(k, dict(h2=h2, h2h=h2h, w_upr=w["w_upr"], bupf=w["bupf"], bupf2=w["bupf2"], wdwf=w["wdwf"], bdwf=w["bdwf"],
                            w_down=w["w_down"], vecs=w["vecs3"], h3=(out if l == depth - 1 else hA)))
            k.pop()
            chk("c")
            if l < depth - 1:
                phase_halo(k, hA, hh, cc_h_src, cc_h_dst)
    except _Stop:
        pass
    k.finish()
    return nc, k


_PROGS = {}
_BUILDERS = {}
_DBG = None


def _run(name, in_maps):
    if name not in _PROGS:
        _PROGS[name] = _BUILDERS[name]()
    nc, k = _PROGS[name]
    res = run_bass_kernel_spmd(nc, in_maps, core_ids=list(range(8)))
    if _DBG is not None:
        _DBG(name, res.results)
    return res.results


def _na_tables(odd):
    def gpos(j):
        j = np.asarray(j)
        m = j - 2048
        pt = np.where(m < 128, 1920 + m, 1792 + (m - 128))
        if not odd:
            return np.where(j < 2048, j, 4095 - pt)
        return np.where(j < 2048, 4095 - j, pt)

    tiles = [(6, 6 + d) for d in (-2, -1, 0, 1, 2)]
    tiles += [(0, kt) for kt in (0, 1, 2, 3)] + [(1, kt) for kt in (0, 1, 2, 3)]
    tiles += [(14, kt) for kt in (12, 13, 14, 15, 16)] + [(15, kt) for kt in (13, 14, 15, 16, 17)]
    dr = np.zeros((23, 128, 128), np.int64)
    dc = np.zeros((23, 128, 128), np.int64)
    ok = np.zeros((23, 128, 128), bool)
    ar = np.arange(128)
    for i, (qp, kt) in enumerate(tiles):
        gk = gpos(kt * 128 + ar)[:, None]
        gq = gpos(qp * 128 + ar)[None, :]
        krow, kcol, qrow, qcol = gk // 64, gk % 64, gq // 64, gq % 64
        rs = np.clip(qrow - 4, 0, 56)
        cs = np.clip(qcol - 8, 0, 48)
        ok[i] = (krow >= rs) & (krow < rs + 8) & (kcol >= cs) & (kcol < cs + 16)
        dr[i] = np.clip(krow - qrow + 7, 0, 14)
        dc[i] = np.clip(kcol - qcol + 15, 0, 30)
    return dr, dc, ok


def kernel(x, mem, ln_in_w, ln_in_b, w_in, b_in, ml_norm_w, na_rpb, w_mix_out, b_mix_out,
           ln1_w, ln1_b, w_xq, w_xkv, w_xo, b_xo, ln2_w, ln2_b,
           w_up, b_up, w_dw, b_dw, w_down, b_down, ln3_w, ln3_b):
    f = lambda a: np.ascontiguousarray(np.asarray(a, dtype=np.float32))
    x, mem = f(x), f(mem)
    cores = range(8)
    consts = np.zeros((128, 4, 128), np.float32)
    ar = np.arange(128)
    consts[:, 0, :] = np.eye(128)
    consts[:, 1, :] = (ar[:, None] <= ar[None, :])
    consts[:, 2, :] = (ar[:, None] >= ar[None, :])
    consts[:, 3, :] = 1.0
    row = lambda v: f(v).reshape(1, -1)

    def loc(c, a):
        return f(a[0:2048]) if c % 2 == 0 else f(a[4095:2047:-1])

    shared = dict(consts=consts, lnw=row(ln_in_w), lnb=row(ln_in_b))
    par = [dict(), dict()]
    tabs = [_na_tables(0), _na_tables(1)]
    for l in range(DEPTH):
        wi, bi = f(w_in[l]), f(b_in[l])
        b_main = np.concatenate([bi[0:2048], bi[2064:3600]])
        gsel = [np.arange(2048, 2064), np.concatenate([np.arange(2056, 2064), np.arange(2048, 2056)])]
        sfx = "_%d" % l
        shared["w_main" + sfx] = f(np.concatenate([wi[:, 0:2048], wi[:, 2064:3600]], axis=1))
        shared["bfm" + sfx] = f(b_main.reshape(28, 128).T)
        shared["mlw" + sfx] = row(ml_norm_w[l])
        shared["w_out" + sfx] = f(w_mix_out[l])
        shared["vecs1" + sfx] = f(np.stack([b_mix_out[l], ln1_w[l], ln1_b[l]]))
        shared["w_xq" + sfx] = f(w_xq[l])
        shared["w_xkv" + sfx] = f(w_xkv[l])
        shared["w_xo" + sfx] = f(w_xo[l])
        shared["vecs2" + sfx] = f(np.stack([b_xo[l], ln2_w[l], ln2_b[l]]))
        shared["w_upr" + sfx] = f(f(w_up[l]).reshape(8, 128, 44, 128).transpose(2, 1, 0, 3))
        bupf = f(f(b_up[l]).reshape(44, 128).T)
        shared["bupf" + sfx] = bupf
        shared["bupf2" + sfx] = f(np.repeat(bupf, 2, axis=1))
        shared["bdwf" + sfx] = f(f(b_dw[l]).reshape(44, 128).T)
        shared["w_down" + sfx] = f(w_down[l])
        shared["vecs3" + sfx] = f(np.stack([b_down[l], ln3_w[l], ln3_b[l]]))
        wd3 = f(w_dw[l]).reshape(3, 44, 128).transpose(2, 1, 0)
        rp = f(na_rpb[l])
        for p in range(2):
            par[p]["w_gate" + sfx] = f(wi[:, gsel[p]])
            par[p]["brow" + sfx] = f(np.concatenate([b_main, bi[gsel[p]]]).reshape(1, -1))
            par[p]["wdwf" + sfx] = f(wd3 if p == 0 else wd3[:, :, ::-1])
            dr, dc, ok = tabs[p]
            t = np.where(ok[None], rp[:, dr, dc], np.float32(NEG)).astype(np.float32)
            par[p]["rpbt" + sfx] = f(t.transpose(0, 2, 1, 3))
    in_maps = []
    for c in cores:
        m = dict(shared)
        m.update(par[c % 2])
        m["x"] = loc(c, x[c // 2])
        m["mem"] = mem[c // 2]
        in_maps.append(m)
    if _DBG is not None and _DBG("in_maps", in_maps):
        return None
    res = _run("fused", in_maps)
    out = np.zeros((4, 4096, D), np.float32)
    for c in cores:
        if c % 2 == 0:
            out[c // 2, 0:2048] = res[c]["out"]
        else:
            out[c // 2, 2048:4096] = res[c]["out"][::-1]
    return out


_BUILDERS.update(fused=build_fused, pre=build_pre, a1=build_a1, a2=build_a2, b1=build_b1, b1b=build_b1b, b2=build_b2, c=build_c)
```
